# Optimizing a Trainium2 kernel written in Bass

```python
import math
import jax, jax.numpy as jnp
from jax import lax
import numpy as np

D_MODEL = 1024
BATCH = 8
SEQ = 8192
DEPTH = 2
DEC_BATCH = 32
DEC_SEQ = 2048
PAST_LEN = 128

N_BRANCH = 4
BR_W = 256
A_HEADS = 4
A_DK = 32
A_DV = 2 * A_DK
B_HEADS = 4
B_DK = 64
B_DV = 64
C_HEADS = 4
C_NOPE = 32
C_ROPE = 16
C_V = 64
C_Q_LORA = 192
C_KV_LORA = 128
ROPE_BASE = 10000.0
D_BLOCKS = 4
D_BLOCK_W = BR_W // D_BLOCKS
CONV_W = 4
CONV_LEFT = 2
RG_C = 8.0
N_EXPERTS = 16
D_EXPERT = 2048
CAPACITY_FACTOR = 2
Q_BLOCK = 128
CHUNK = 128
LN_EPS = 1e-5
RMS_EPS = 1e-6
ALPHA = (2 * DEPTH) ** 0.25
BETA = (8 * DEPTH) ** -0.25
ALIBI_SLOPES = tuple(2.0 ** (-8.0 * (h + 1) / A_HEADS) for h in range(A_HEADS))

_IN_WIDTHS = (
    A_HEADS * 2 * A_DK,
    A_HEADS * 2 * A_DK,
    A_HEADS * A_DV,
    B_HEADS * B_DK,
    B_HEADS * B_DK,
    B_HEADS * B_DK,
    B_HEADS * B_DV,
    B_HEADS * B_DV,
    C_Q_LORA,
    C_KV_LORA,
    C_ROPE,
    BR_W,
    BR_W,
    N_BRANCH * D_MODEL,
)
IN_W = sum(_IN_WIDTHS)
IN_SPLITS = tuple(int(s) for s in np.cumsum(_IN_WIDTHS)[:-1])

kernel_name = 'hybrid_bidir_encoder_gated_merge'


def layer_norm(x, g, b):
    xf = x.astype(jnp.float32)
    mu = jnp.mean(xf, axis=-1, keepdims=True)
    var = jnp.mean(jnp.square(xf - mu), axis=-1, keepdims=True)
    return ((xf - mu) * lax.rsqrt(var + LN_EPS) * g + b).astype(x.dtype)


def rms_norm(x, g):
    xf = x.astype(jnp.float32)
    return (xf * lax.rsqrt(jnp.mean(xf * xf, axis=-1, keepdims=True) + RMS_EPS) * g).astype(x.dtype)


def sweep_query_blocks(fn, q_arrays):
    S = q_arrays[0].shape[1]
    nb = S // Q_BLOCK
    blocks = tuple(jnp.moveaxis(a.reshape(a.shape[0], nb, Q_BLOCK, *a.shape[2:]), 1, 0) for a in q_arrays)
    starts = jnp.arange(nb, dtype=jnp.int32) * Q_BLOCK
    out = lax.map(lambda xs: fn(*xs), (blocks, starts))
    out = jnp.moveaxis(out, 0, 1)
    return out.reshape(out.shape[0], S, *out.shape[3:])


def diff_attention(q, k, v, lam_params, subln_g, layer_idx):
    B, S = k.shape[0], k.shape[1]
    lambda_init = 0.8 - 0.6 * math.exp(-0.3 * layer_idx)
    lp = lam_params.astype(jnp.float32)
    lam = jnp.exp(jnp.sum(lp[0] * lp[1])) - jnp.exp(jnp.sum(lp[2] * lp[3])) + lambda_init
    slopes = jnp.asarray(ALIBI_SLOPES, dtype=jnp.float32)
    k_pos = jnp.arange(S, dtype=jnp.int32)
    scale = A_DK ** -0.5

    def block(qs, start):
        (qb,) = qs
        q_pos = start + jnp.arange(Q_BLOCK, dtype=jnp.int32)
        dist = jnp.abs(q_pos[:, None] - k_pos[None, :]).astype(jnp.float32)
        bias = -slopes[:, None, None] * dist
        s = jnp.einsum('bqhmd,bkhmd->bhmqk', qb, k).astype(jnp.float32) * scale + bias[None, :, None]
        p = jax.nn.softmax(s, axis=-1)
        w = p[:, :, 0] - lam * p[:, :, 1]
        return jnp.einsum('bhqk,bkhd->bqhd', w.astype(v.dtype), v)

    o = sweep_query_blocks(block, (q,))
    o = rms_norm(o, subln_g) * (1.0 - lambda_init)
    return o.reshape(B, S, A_HEADS * A_DV)


def rope_tables(S, d):
    inv = ROPE_BASE ** (-jnp.arange(0, d, 2, dtype=jnp.float32) / d)
    ang = jnp.arange(S, dtype=jnp.float32)[:, None] * inv[None, :]
    return jnp.cos(ang), jnp.sin(ang)


def apply_rope(x, cos, sin):
    half = x.shape[-1] // 2
    x1, x2 = x[..., :half], x[..., half:]
    c, s = cos.astype(x.dtype), sin.astype(x.dtype)
    return jnp.concatenate([x1 * c - x2 * s, x1 * s + x2 * c], axis=-1)


def mla_attention(c_q, c_kv, k_r, q_norm_g, w_uq, kv_norm_g, w_ukv):
    B, S, _ = c_q.shape
    q = (rms_norm(c_q, q_norm_g) @ w_uq).reshape(B, S, C_HEADS, C_NOPE + C_ROPE)
    kv = (rms_norm(c_kv, kv_norm_g) @ w_ukv).reshape(B, S, C_HEADS, C_NOPE + C_V)
    q_nope, q_rope = q[..., :C_NOPE], q[..., C_NOPE:]
    k_nope, v = kv[..., :C_NOPE], kv[..., C_NOPE:]
    cos, sin = rope_tables(S, C_ROPE)
    q_rope = apply_rope(q_rope, cos[:, None, :], sin[:, None, :])
    k_rope = apply_rope(k_r, cos, sin)
    scale = (C_NOPE + C_ROPE) ** -0.5

    def block(qs, start):
        qn, qr = qs
        s = (jnp.einsum('bqhd,bkhd->bhqk', qn, k_nope)
             + jnp.einsum('bqhd,bkd->bhqk', qr, k_rope)).astype(jnp.float32) * scale
        p = jax.nn.softmax(s, axis=-1)
        return jnp.einsum('bhqk,bkhd->bqhd', p.astype(v.dtype), v)

    o = sweep_query_blocks(block, (q_nope, q_rope))
    return o.reshape(B, S, C_HEADS * C_V)


def chunked_gated_recurrence(q, k, v, log_f):
    B, S, H, DK = q.shape
    DV = v.shape[-1]
    n = S // CHUNK

    def to_chunks(a):
        return a.astype(jnp.float32).reshape(B, n, CHUNK, H, a.shape[-1]).transpose(1, 0, 3, 2, 4)

    qc, kc, vc, fc = (to_chunks(a) for a in (q, k, v, log_f))
    tri = jnp.tril(jnp.ones((CHUNK, CHUNK), dtype=bool))[:, :, None]

    def step(state, xs):
        qi, ki, vi, fi = xs
        b = jnp.cumsum(fi, axis=2)
        diff = b[:, :, :, None, :] - b[:, :, None, :, :]
        decay = jnp.exp(jnp.where(tri, diff, -jnp.inf))
        scores = jnp.einsum('bhtd,bhsd,bhtsd->bhts', qi, ki, decay)
        o = scores @ vi + jnp.einsum('bhtd,bhdv->bhtv', qi * jnp.exp(b), state)
        b_last = b[:, :, -1:, :]
        new_state = (jnp.exp(b_last[:, :, 0, :])[..., None] * state
                     + jnp.einsum('bhsd,bhsv->bhdv', ki * jnp.exp(b_last - b), vi))
        return new_state, o

    state0 = jnp.zeros((B, H, DK, DV), jnp.float32)
    _, o = lax.scan(step, state0, (qc, kc, vc, fc))
    return o.transpose(1, 0, 3, 2, 4).reshape(B, S, H, DV)


def hgrn2_direction(q, v, f_logits, lb, reverse):
    B, S, H, DK = q.shape
    f = (lb + (1.0 - lb) * jax.nn.sigmoid(f_logits.astype(jnp.float32))).reshape(B, S, H, DK)
    k = 1.0 - f
    log_f = jnp.log(f)
    if reverse:
        q, k, v, log_f = (jnp.flip(a, axis=1) for a in (q, k, v, log_f))
    o = chunked_gated_recurrence(q, k, v, log_f)
    return jnp.flip(o, axis=1) if reverse else o


def hgrn2_mixer(b_q, b_ff, b_fb, b_i, b_g, lb, norm_g):
    B, S, _ = b_q.shape
    q = jax.nn.silu(b_q.astype(jnp.float32)).reshape(B, S, B_HEADS, B_DK)
    v = b_i.astype(jnp.float32).reshape(B, S, B_HEADS, B_DV)
    o = hgrn2_direction(q, v, b_ff, lb[0], False) + hgrn2_direction(q, v, b_fb, lb[1], True)
    o = rms_norm(o, norm_g.reshape(B_HEADS, B_DV)).reshape(B, S, B_HEADS * B_DV)
    return (o * jax.nn.silu(b_g.astype(jnp.float32))).astype(b_q.dtype)


def linear_scan_combine(e1, e2):
    a1, b1 = e1
    a2, b2 = e2
    return a1 * a2, a2 * b1 + b2


def rglru_mixer(d_x, d_g, conv_w, conv_b, w_a, b_a, w_x, b_x, lam):
    B, S, W = d_x.shape
    pad = jnp.pad(d_x, ((0, 0), (CONV_LEFT, CONV_W - 1 - CONV_LEFT), (0, 0)))
    xc = (sum(pad[:, j:j + S] * conv_w[j] for j in range(CONV_W)) + conv_b).astype(jnp.float32)
    xb = xc.reshape(B, S, D_BLOCKS, D_BLOCK_W)
    h_sum = jnp.zeros_like(xc)
    for d, reverse in ((0, False), (1, True)):
        r = jax.nn.sigmoid(jnp.einsum('bsnc,ncd->bsnd', xb, w_a[d]).reshape(B, S, W) + b_a[d])
        i = jax.nn.sigmoid(jnp.einsum('bsnc,ncd->bsnd', xb, w_x[d]).reshape(B, S, W) + b_x[d])
        log_a = -RG_C * r * jax.nn.softplus(-lam[d].astype(jnp.float32))
        a = jnp.exp(log_a)
        u = jnp.sqrt(-jnp.expm1(2.0 * log_a)) * (i * xc)
        _, h = lax.associative_scan(linear_scan_combine, (a, u), axis=1, reverse=reverse)
        h_sum = h_sum + h
    return (h_sum * jax.nn.gelu(d_g.astype(jnp.float32))).astype(d_x.dtype)


def expert_choice_ffn(x, w_router, w_gate, w_up, w_down):
    B, S, D = x.shape
    T = B * S
    cap = CAPACITY_FACTOR * T // N_EXPERTS
    xt = x.reshape(T, D)
    aff = jax.nn.softmax((xt @ w_router).astype(jnp.float32), axis=-1)
    g, idx = lax.top_k(aff.T, cap)
    xe = xt[idx]
    h = jax.nn.silu(jnp.einsum('ecd,edf->ecf', xe, w_gate)) * jnp.einsum('ecd,edf->ecf', xe, w_up)
    ye = jnp.einsum('ecf,efd->ecd', h, w_down) * g[..., None].astype(x.dtype)
    y = jnp.zeros((T, D), ye.dtype).at[idx.reshape(-1)].add(ye.reshape(-1, D))
    return y.reshape(B, S, D).astype(x.dtype)


def token_mixer(x, l, p):
    B, S, _ = x.shape
    u = x @ p['w_in'][l]
    (a_q, a_k, a_v, b_q, b_ff, b_fb, b_i, b_g,
     c_q, c_kv, c_kr, d_x, d_g, gate) = jnp.split(u, IN_SPLITS, axis=-1)
    y_a = diff_attention(a_q.reshape(B, S, A_HEADS, 2, A_DK), a_k.reshape(B, S, A_HEADS, 2, A_DK),
                         a_v.reshape(B, S, A_HEADS, A_DV), p['diff_lambda'][l], p['diff_subln_g'][l], l)
    y_b = hgrn2_mixer(b_q, b_ff, b_fb, b_i, b_g, p['hgrn_lb'][l], p['hgrn_norm_g'][l])
    y_c = mla_attention(c_q, c_kv, c_kr, p['mla_q_norm_g'][l], p['mla_w_uq'][l],
                        p['mla_kv_norm_g'][l], p['mla_w_ukv'][l])
    y_d = rglru_mixer(d_x, d_g, p['rg_conv_w'][l], p['rg_conv_b'][l], p['rg_w_a'][l], p['rg_b_a'][l],
                      p['rg_w_x'][l], p['rg_b_x'][l], p['rg_lambda'][l])
    gate = jax.nn.sigmoid(gate.reshape(B, S, N_BRANCH, D_MODEL))
    mix = sum(gate[:, :, i] * (y @ p['w_branch'][l, i]) for i, y in enumerate((y_a, y_b, y_c, y_d)))
    return mix @ p['w_out'][l]


def encoder_trunk(x, p):
    for l in range(DEPTH):
        x = layer_norm(ALPHA * x + token_mixer(x, l, p), p['ln_g'][l, 0], p['ln_b'][l, 0])
        y = expert_choice_ffn(x, p['w_router'][l], p['w_e_gate'][l], p['w_e_up'][l], p['w_e_down'][l])
        x = layer_norm(ALPHA * x + y, p['ln_g'][l, 1], p['ln_b'][l, 1])
    return x


def setup_inputs(seed: int = 0) -> dict:
    key = jax.random.key(seed)
    ks = jax.random.split(key, 26)
    f32 = jnp.float32
    L = DEPTH

    def nrm(k, shape, scale):
        return jax.random.normal(k, shape, f32) * scale

    def gain(k, shape):
        return 1.0 + nrm(k, shape, 0.02)

    a0 = jax.random.uniform(ks[17], (L, 2, BR_W), f32, 0.9, 0.999)
    p_lam = a0 ** (1.0 / RG_C)
    return {
        'x_prompt': nrm(ks[0], (BATCH, SEQ, D_MODEL), 1.0),
        'x_sample': nrm(ks[1], (DEC_BATCH, DEC_SEQ, D_MODEL), 1.0),
        'w_in': nrm(ks[2], (L, D_MODEL, IN_W), D_MODEL ** -0.5),
        'diff_lambda': nrm(ks[3], (L, 4, A_DK), 0.1),
        'diff_subln_g': gain(ks[4], (L, A_DV)),
        'hgrn_lb_logits': nrm(ks[5], (L, 2, B_HEADS * B_DK), 0.5),
        'hgrn_norm_g': gain(ks[6], (L, B_HEADS * B_DV)),
        'mla_q_norm_g': gain(ks[7], (L, C_Q_LORA)),
        'mla_w_uq': nrm(ks[8], (L, C_Q_LORA, C_HEADS * (C_NOPE + C_ROPE)), C_Q_LORA ** -0.5),
        'mla_kv_norm_g': gain(ks[9], (L, C_KV_LORA)),
        'mla_w_ukv': nrm(ks[10], (L, C_KV_LORA, C_HEADS * (C_NOPE + C_V)), C_KV_LORA ** -0.5),
        'rg_conv_w': nrm(ks[11], (L, CONV_W, BR_W), CONV_W ** -0.5),
        'rg_conv_b': nrm(ks[12], (L, BR_W), 0.02),
        'rg_w_a': nrm(ks[13], (L, 2, D_BLOCKS, D_BLOCK_W, D_BLOCK_W), D_BLOCK_W ** -0.5),
        'rg_b_a': nrm(ks[14], (L, 2, BR_W), 0.02),
        'rg_w_x': nrm(ks[15], (L, 2, D_BLOCKS, D_BLOCK_W, D_BLOCK_W), D_BLOCK_W ** -0.5),
        'rg_b_x': nrm(ks[16], (L, 2, BR_W), 0.02),
        'rg_lambda': jnp.log(p_lam) - jnp.log1p(-p_lam),
        'w_branch': nrm(ks[18], (L, N_BRANCH, BR_W, D_MODEL), BR_W ** -0.5),
        'w_out': nrm(ks[19], (L, D_MODEL, D_MODEL), BETA * D_MODEL ** -0.5),
        'ln_g': gain(ks[20], (L, 2, D_MODEL)),
        'ln_b': nrm(ks[21], (L, 2, D_MODEL), 0.02),
        'w_router': nrm(ks[22], (L, D_MODEL, N_EXPERTS), D_MODEL ** -0.5),
        'w_e_gate': nrm(ks[23], (L, N_EXPERTS, D_MODEL, D_EXPERT), D_MODEL ** -0.5),
        'w_e_up': nrm(ks[24], (L, N_EXPERTS, D_MODEL, D_EXPERT), D_MODEL ** -0.5),
        'w_e_down': nrm(ks[25], (L, N_EXPERTS, D_EXPERT, D_MODEL), BETA * D_EXPERT ** -0.5),
    }


def reference(x_prompt, x_sample, w_in, diff_lambda, diff_subln_g, hgrn_lb_logits, hgrn_norm_g,
              mla_q_norm_g, mla_w_uq, mla_kv_norm_g, mla_w_ukv, rg_conv_w, rg_conv_b, rg_w_a, rg_b_a,
              rg_w_x, rg_b_x, rg_lambda, w_branch, w_out, ln_g, ln_b, w_router, w_e_gate, w_e_up,
              w_e_down):
    lb_all = jnp.cumsum(jax.nn.softmax(hgrn_lb_logits.astype(jnp.float32), axis=0), axis=0)
    lb_all = lb_all - lb_all[:1]
    params = dict(w_in=w_in, diff_lambda=diff_lambda, diff_subln_g=diff_subln_g, hgrn_lb=lb_all,
                  hgrn_norm_g=hgrn_norm_g, mla_q_norm_g=mla_q_norm_g, mla_w_uq=mla_w_uq,
                  mla_kv_norm_g=mla_kv_norm_g, mla_w_ukv=mla_w_ukv, rg_conv_w=rg_conv_w,
                  rg_conv_b=rg_conv_b, rg_w_a=rg_w_a, rg_b_a=rg_b_a, rg_w_x=rg_w_x, rg_b_x=rg_b_x,
                  rg_lambda=rg_lambda, w_branch=w_branch, w_out=w_out, ln_g=ln_g, ln_b=ln_b,
                  w_router=w_router, w_e_gate=w_e_gate, w_e_up=w_e_up, w_e_down=w_e_down)
    y_prompt = encoder_trunk(x_prompt, params)
    y_sample = encoder_trunk(x_sample, params)
    return (y_prompt, y_sample)
```

```python
import math
from contextlib import ExitStack
import numpy as np
import ml_dtypes
import concourse.bass as bass
import concourse.mybir as mybir
from concourse.bass_utils import run_bass_kernel_spmd

F32 = mybir.dt.float32
BF16 = mybir.dt.bfloat16
I32 = mybir.dt.int32
AF = mybir.ActivationFunctionType
ALU = mybir.AluOpType
AX = mybir.AxisListType

NCORES = 8
D = 1024
IN_W = 6992
NE = 16
DE = 2048
LN_EPS = 1e-5
RMS_EPS = 1e-6
SLOPES = [2.0 ** (-2.0 * (h + 1)) for h in range(4)]
O_AQ, O_AK, O_AV = 0, 256, 512
O_BQ, O_BFF, O_BFB, O_BI, O_BG = 768, 1024, 1280, 1536, 1792
O_CQ, O_CKV, O_CKR = 2048, 2240, 2368
O_DX, O_DG, O_GATE = 2384, 2640, 2896
HC = 64


class Prog:
    CLASSES = {"q_sp": ("sp", 4), "q_st": ("sp", 4), "q_pool": ("pool", 4), "q_gather": ("pool", 2), "q_scat": ("pool", 2)}

    def __init__(self, nc, ctx):
        self.nc = nc
        self.ctx = ctx
        self.eng = {"pe": nc.tensor, "dve": nc.vector, "act": nc.scalar, "pool": nc.gpsimd, "sp": nc.sync}
        self.streams = ["pe", "dve", "act", "pool"]
        self.issuer_of = {s: s for s in self.streams}
        for c, (iss, k) in self.CLASSES.items():
            for i in range(k):
                self.streams.append(f"{c}#{i}")
                self.issuer_of[f"{c}#{i}"] = iss
        self.sem = {s: ctx.enter_context(nc.semaphore("s_" + s.replace("#", "_"))) for s in self.streams}
        self.cnt = {s: 0 for s in self.streams}
        self.rr = {c: 0 for c in self.CLASSES}
        self.waited = {}
        self.last_w = {}
        self.readers = {}
        self.n_inst = 0
        self.gen = 0

    def _need(self, issuer, deps):
        best = {}
        for (s, c) in deps:
            if c > best.get(s, 0):
                best[s] = c
        for s, c in best.items():
            if self.waited.get((issuer, s), 0) >= c:
                continue
            self.eng[issuer].wait_ge(self.sem[s], c)
            self.waited[(issuer, s)] = c
            self.n_inst += 1

    def op(self, stream, fn, reads=(), writes=()):
        if stream in self.CLASSES:
            k = self.CLASSES[stream][1]
            sub = f"{stream}#{self.rr[stream] % k}"
            self.rr[stream] += 1
            stream = sub
        issuer = self.issuer_of[stream]
        deps = []
        for k in reads:
            if k in self.last_w:
                deps.append(self.last_w[k])
        for k in writes:
            if k in self.last_w:
                deps.append(self.last_w[k])
            deps.extend(self.readers.get(k, ()))
        self._need(issuer, deps)
        inst = fn(self.eng[issuer])
        inc = 16 if "#" in stream else 1
        self.cnt[stream] += inc
        inst.then_inc(self.sem[stream], inc)
        me = (stream, self.cnt[stream])
        for k in writes:
            self.last_w[k] = me
            self.readers[k] = []
        for k in reads:
            if k not in writes:
                lst = self.readers.setdefault(k, [])
                lst[:] = [r for r in lst if r[0] != stream]
                lst.append(me)
        self.n_inst += 1
        return inst

    def barrier(self):
        deps = [(s, self.cnt[s]) for s in self.streams if self.cnt[s] > 0]
        for issuer in ("sp", "pool", "act", "dve", "pe"):
            self._need(issuer, deps)
        self.last_w = {}
        self.readers = {}
        for s in self.streams:
            if self.cnt[s] > 50000:
                self.gen += 1
                self.sem[s] = self.ctx.enter_context(self.nc.semaphore(f"s_{s.replace('#', '_')}_{self.gen}"))
                self.cnt[s] = 0
                for k in list(self.waited):
                    if k[1] == s:
                        del self.waited[k]

    def dma(self, q, out, in_, reads=(), writes=(), **kw):
        if q == "q_sp" and "DRam" in type(out.tensor).__name__:
            q = "q_st"
        return self.op(q, lambda e: e.dma_start(out=out, in_=in_, **kw), reads, writes)

    def mm(self, out, lhsT, rhs, start, stop, reads=(), writes=()):
        return self.op("pe", lambda e: e.matmul(out, lhsT=lhsT, rhs=rhs, start=start, stop=stop), reads, writes)

    def tr(self, out, in_, ident, reads=(), writes=()):
        return self.op("pe", lambda e: e.transpose(out, in_, ident), reads, writes)

    def act(self, out, in_, func, reads=(), writes=(), **kw):
        return self.op("act", lambda e: e.activation(out=out, in_=in_, func=func, **kw), reads, writes)

    def tt(self, eng, out, in0, in1, op, reads=(), writes=()):
        return self.op(eng, lambda e: e.tensor_tensor(out=out, in0=in0, in1=in1, op=op), reads, writes)

    def ts(self, eng, out, in0, s1, s2, op0, op1=None, reads=(), writes=()):
        if op1 is None:
            return self.op(eng, lambda e: e.tensor_scalar(out=out, in0=in0, scalar1=s1, scalar2=None, op0=op0), reads, writes)
        return self.op(eng, lambda e: e.tensor_scalar(out=out, in0=in0, scalar1=s1, scalar2=s2, op0=op0, op1=op1), reads, writes)

    def stt(self, out, in0, scalar, in1, op0, op1, reads=(), writes=()):
        return self.op("dve", lambda e: e.scalar_tensor_tensor(out=out, in0=in0, scalar=scalar, in1=in1, op0=op0, op1=op1), reads, writes)

    def copy(self, eng, out, in_, reads=(), writes=()):
        if eng == "act":
            return self.op("act", lambda e: e.copy(out=out, in_=in_), reads, writes)
        return self.op(eng, lambda e: e.tensor_copy(out=out, in_=in_), reads, writes)

    def memset(self, eng, ap, val, writes=()):
        return self.op(eng, lambda e: e.memset(ap, val), (), writes)


class Ring:
    def __init__(self, tiles, name):
        self.tiles = tiles
        self.name = name
        self.i = -1

    def next(self):
        self.i = (self.i + 1) % len(self.tiles)
        return self.tiles[self.i], f"{self.name}{self.i}"


class Grp:
    def __init__(self, name, nseq, S):
        self.name = name
        self.nseq = nseq
        self.S = S
        self.T = nseq * S
        self.J = self.T // 128
        m = self.T // 8
        self.cmax = 128 * int(math.ceil((m + 8.0 * math.sqrt(m)) / 128.0))
        self.cap = self.T


def col_chunks():
    ch = [(c, 128) for c in range(0, 2816, 128)] + [(2816, 80)] + [(O_GATE + 128 * i, 128) for i in range(32)]
    return ch


COL_GROUPS = [(0, 1024), (1024, 1024), (2048, 848)] + [(O_GATE + 1024 * i, 1024) for i in range(4)]


def build_program(groups, depth, debug=False):
    nc = bass.Bass("TRN2", target_bir_lowering=False)
    dt = lambda name, shape, dtype, kind="Internal": nc.dram_tensor(name, list(shape), dtype, kind=kind).ap()
    L = depth
    W = {}
    W["w_in"] = dt("w_in", [L, D, IN_W], F32, "ExternalInput")
    W["diff_lambda"] = dt("diff_lambda", [L, 4, 32], F32, "ExternalInput")
    W["diff_subln_g"] = dt("diff_subln_g", [L, 64], F32, "ExternalInput")
    W["hgrn_lb_logits"] = dt("hgrn_lb_logits", [L, 2, 256], F32, "ExternalInput")
    W["hgrn_norm_g"] = dt("hgrn_norm_g", [L, 256], F32, "ExternalInput")
    W["mla_q_norm_g"] = dt("mla_q_norm_g", [L, 192], F32, "ExternalInput")
    W["mla_w_uq"] = dt("mla_w_uq", [L, 192, 192], F32, "ExternalInput")
    W["mla_kv_norm_g"] = dt("mla_kv_norm_g", [L, 128], F32, "ExternalInput")
    W["mla_w_ukv"] = dt("mla_w_ukv", [L, 128, 384], F32, "ExternalInput")
    W["rg_conv_w"] = dt("rg_conv_w", [L, 4, 256], F32, "ExternalInput")
    W["rg_conv_b"] = dt("rg_conv_b", [L, 256], F32, "ExternalInput")
    W["rg_w_a"] = dt("rg_w_a", [L, 2, 4, 64, 64], F32, "ExternalInput")
    W["rg_b_a"] = dt("rg_b_a", [L, 2, 256], F32, "ExternalInput")
    W["rg_w_x"] = dt("rg_w_x", [L, 2, 4, 64, 64], F32, "ExternalInput")
    W["rg_b_x"] = dt("rg_b_x", [L, 2, 256], F32, "ExternalInput")
    W["rg_lambda"] = dt("rg_lambda", [L, 2, 256], F32, "ExternalInput")
    W["w_branch"] = dt("w_branch", [L, 4, 256, D], F32, "ExternalInput")
    W["w_out"] = dt("w_out", [L, D, D], F32, "ExternalInput")
    W["ln_g"] = dt("ln_g", [L, 2, D], F32, "ExternalInput")
    W["ln_b"] = dt("ln_b", [L, 2, D], F32, "ExternalInput")
    W["w_router"] = dt("w_router", [L, D, NE], F32, "ExternalInput")
    W["w_e_gate"] = dt("w_e_gate", [L, NE, D, DE], F32, "ExternalInput")
    W["w_e_up"] = dt("w_e_up", [L, NE, D, DE], F32, "ExternalInput")
    W["w_e_down"] = dt("w_e_down", [L, NE, DE, D], F32, "ExternalInput")
    C = {}
    C["ident"] = dt("c_ident", [128, 128], F32, "ExternalInput")
    C["qaug"] = dt("c_qaug", [4, 2, 512], F32, "ExternalInput")
    C["acol"] = dt("c_acol", [128, 4, 127], F32, "ExternalInput")
    C["aband"] = dt("c_aband", [128, 4, 896], F32, "ExternalInput")
    C["hmask"] = dt("c_hmask", [64, 2, 4, 64], F32, "ExternalInput")
    C["ltri"] = dt("c_ltri", [128, 128], F32, "ExternalInput")
    Smax = max(g.S for g in groups)
    C["rope"] = dt("c_rope", [4, 48, Smax], F32, "ExternalInput")
    for g in groups:
        g.x_in = dt("x_" + g.name, [g.T, D], F32, "ExternalInput")
        g.y_out = dt("y_" + g.name, [g.T, D], F32, "ExternalOutput")
        g.tokid = dt("c_tokid_" + g.name, [128, g.J], I32, "ExternalInput")
        g.iotaS = dt("c_iotaS_" + g.name, [128, g.cmax], F32, "ExternalInput")
        g.trash = dt("c_trash_" + g.name, [128, g.cmax // 128], F32, "ExternalInput")
        g.uT = dt("uT_" + g.name, [IN_W, g.T], BF16)
        g.vA = dt("vA_" + g.name, [g.T, 256], BF16)
        g.vB = dt("vB_" + g.name, [g.T, 256], BF16)
        g.yT = dt("yT_" + g.name, [1024, g.T], BF16)
        g.oFB = dt("oFB_" + g.name, [2, 256, g.T], F32)
        g.xs = dt("xs_" + g.name, [g.T, D], F32)
        g.x1b = dt("x1b_" + g.name, [g.T + g.cmax, D], BF16)
        g.affL = dt("affL_" + g.name, [g.T, NE], F32)
        g.affG = dt("affG_" + g.name, [NCORES * g.T, NE], F32)
        g.yacc = dt("yacc_" + g.name, [g.T + g.cmax, D], F32)
    dbg = {}

    with ExitStack() as top:
        p = Prog(nc, top)
        uid = [0]

        def sbt(ctx, name, shape, dtype):
            uid[0] += 1
            return ctx.enter_context(nc.sbuf_tensor(f"{name}_u{uid[0]}", list(shape), dtype))
        ps = [top.enter_context(nc.psum_tensor(f"ps{i}", [128, 512], F32)) for i in range(8)]
        PK = [f"ps{i}" for i in range(8)]
        identF = sbt(top, "identF", [128, 128], F32)
        identB = sbt(top, "identB", [128, 128], BF16)
        onesF = sbt(top, "onesF", [128, 128], F32)
        onesB = sbt(top, "onesB", [128, 128], BF16)
        epsc = sbt(top, "epsc", [128, 2], F32)
        p.dma("q_sp", identF[:], C["ident"], writes=["identF"])
        p.copy("dve", identB[:], identF[:], reads=["identF"], writes=["identB"])
        p.memset("dve", onesF[:], 1.0, writes=["onesF"])
        p.memset("dve", onesB[:], 1.0, writes=["onesB"])
        p.memset("dve", epsc[:, 0:1], RMS_EPS, writes=["epsc"])
        p.memset("dve", epsc[:, 1:2], LN_EPS, writes=["epsc"])
        CONSTK = ["identF", "identB", "onesF", "onesB", "epsc"]

        def col(ctx, name, src_vec, n):
            t = sbt(ctx, name, [128, n], F32)
            p.dma("q_sp", t[:].rearrange("p (c o) -> p c o", o=1), src_vec.rearrange("(c p o) -> p c o", p=128, o=1), writes=[name], allow_slow_non_contiguous=True)
            return t

        def rstd_from(out, in_, scale, eps_col, rk, wk, tmpname=None):
            p.act(out, in_, AF.Ln, reads=rk + ["epsc"], writes=wk, scale=scale, bias=eps_col)
            p.act(out, out, AF.Exp, reads=wk, writes=wk, scale=-0.5)

        def phase_A(l, g, src):
            T = g.T
            SUP = min(T, 2048)
            with ExitStack() as ctx:
                xtok = Ring([sbt(ctx, f"A_xtok{i}", [128, 1024], F32) for i in range(2)], "A_xtok")
                xT = sbt(ctx, "A_xT", [128, 8, SUP], BF16)
                wr = Ring([sbt(ctx, f"A_w{i}", [128, 8, 1024], BF16) for i in range(2)], "A_w")
                ost = Ring([sbt(ctx, f"A_o{i}", [128, 512], BF16) for i in range(4)], "A_o")
                ost2 = Ring([sbt(ctx, f"A_p{i}", [128, 256], BF16) for i in range(2)], "A_p")
                chunks = col_chunks()
                pi = 0
                for s0 in range(0, T, SUP):
                    for t0 in range(0, SUP, 128):
                        xt, xk = xtok.next()
                        p.dma("q_sp", xt[:], src[s0 + t0:s0 + t0 + 128, :], writes=[xk])
                        for b in range(2):
                            for j in range(4):
                                kc = b * 4 + j
                                p.tr(ps[b][:, j * 128:(j + 1) * 128], xt[:, kc * 128:(kc + 1) * 128], identF[:],
                                     reads=[xk, "identF"], writes=[PK[b]])
                            p.copy("dve" if b == 0 else "act", xT[:, b * 4:(b + 1) * 4, t0:t0 + 128],
                                   ps[b][:].rearrange("p (j t) -> p j t", j=4), reads=[PK[b]], writes=["A_xT"])
                    for gi, (c0, ncol) in enumerate(COL_GROUPS):
                        wt, wk = wr.next()
                        p.dma("q_pool", wt[:, :, 0:ncol], W["w_in"][l, :, c0:c0 + ncol].rearrange("(kc p) n -> p kc n", p=128),
                              writes=[wk])
                        for tt0 in range(0, SUP, 512):
                            for (cc, cw) in chunks:
                                if not (c0 <= cc < c0 + ncol):
                                    continue
                                b = 2 + (pi % 4)
                                pi += 1
                                for kc in range(8):
                                    p.mm(ps[b][0:cw, :], wt[:, kc, cc - c0:cc - c0 + cw], xT[:, kc, tt0:tt0 + 512], kc == 0, kc == 7,
                                         reads=[wk, "A_xT"], writes=[PK[b]])
                                ot, ok = ost.next()
                                if cc >= O_GATE:
                                    p.act(ot[0:cw, :], ps[b][0:cw, :], AF.Sigmoid, reads=[PK[b]], writes=[ok])
                                elif cc < 256:
                                    p.ts("dve", ot[0:cw, :], ps[b][0:cw, :], 32.0 ** -0.5, None, ALU.mult, reads=[PK[b]], writes=[ok])
                                else:
                                    p.copy("dve" if (pi % 2) else "act", ot[0:cw, :], ps[b][0:cw, :], reads=[PK[b]], writes=[ok])
                                p.dma("q_sp", g.uT[cc:cc + cw, s0 + tt0:s0 + tt0 + 512], ot[0:cw, :], reads=[ok], writes=())
                        if gi in (0, 1):
                            dst = g.vA if gi == 0 else g.vB
                            for t0 in range(0, SUP, 128):
                                b = 6 + ((t0 // 128) % 2)
                                for kc in range(8):
                                    p.mm(ps[b][:, 0:256], xT[:, kc, t0:t0 + 128], wt[:, kc, 512:768], kc == 0, kc == 7,
                                         reads=[wk, "A_xT"], writes=[PK[b]])
                                ot, ok = ost2.next()
                                p.copy("act", ot[:], ps[b][:, 0:256], reads=[PK[b]], writes=[ok])
                                p.dma("q_sp", dst[s0 + t0:s0 + t0 + 128, :], ot[:], reads=[ok], writes=())
            p.barrier()

        def attn_core(ctx, S, maps, post, pref):
            nq = S // 512
            nk = S // 128
            pT = Ring([sbt(ctx, f"{pref}_pT{i}", [128, 512], BF16) for i in range(4)], pref + "_pT")
            tmpF = Ring([sbt(ctx, f"{pref}_tf{i}", [128, 512], F32) for i in range(2)], pref + "_tf")
            sring = Ring([ps[0], ps[1], ps[2]], "psS")
            oring = Ring([ps[3], ps[4]], "psO")
            skeys = {"psS0": PK[0], "psS1": PK[1], "psS2": PK[2], "psO0": PK[3], "psO1": PK[4]}
            for qt in range(nq):
                for mi, m in enumerate(maps):
                    O, okr = oring.next()
                    ok = skeys[okr]
                    pend = None
                    for kt in range(nk + 1):
                        cur = None
                        if kt < nk:
                            Sb, skr = sring.next()
                            sk = skeys[skr]
                            Dd = qt * 512 - kt * 128
                            h = m["head"]
                            if h is None:
                                cls = "plain"
                            elif Dd >= 128:
                                cls = "below"
                            elif Dd <= -512:
                                cls = "above"
                            else:
                                cls = "diag"
                            if cls in ("plain",):
                                p.mm(Sb[:, :], m["k"](kt, 0), m["q"](qt), True, True, reads=m["keys"], writes=[sk])
                            elif cls == "diag":
                                p.mm(Sb[:, :], m["kd"](kt), m["qd"](qt), True, True, reads=m["keys"], writes=[sk])
                            else:
                                p.mm(Sb[:, :], m["k"](kt, 0 if cls == "below" else 1), m["q"](qt), True, True, reads=m["keys"], writes=[sk])
                            cur = (Sb, sk, cls, Dd, kt)
                        if pend is not None:
                            Sb2, sk2, cls2, Dd2, kt2 = pend
                            pt, pk = pT.next()
                            if cls2 == "plain":
                                p.act(pt[:], Sb2[:, :], AF.Exp, reads=[sk2], writes=[pk])
                            elif cls2 == "diag":
                                tf, tk = tmpF.next()
                                off = Dd2 + 384
                                p.tt("dve", tf[:], Sb2[:, :], m["aband"][:, m["head"], off:off + 512], ALU.add,
                                     reads=[sk2, "aband"], writes=[tk])
                                p.act(pt[:], tf[:], AF.Exp, reads=[tk], writes=[pk])
                            else:
                                jd = Dd2 // 128
                                p.act(pt[:], Sb2[:, :], AF.Exp, reads=[sk2, "acol"], writes=[pk],
                                      bias=m["acol"][:, m["head"], jd + 63:jd + 64])
                            p.mm(O[0:65, :], m["v"](kt2), pt[:], kt2 == 0, kt2 == nk - 1, reads=[pk] + m["vkeys"], writes=[ok])
                        pend = cur
                    post(qt, mi, O, ok)

        def attn_norm(ctx_tiles, O, ok, dst, dk):
            r1, bcS = ctx_tiles
            p.op("dve", lambda e: e.reciprocal(out=r1[64:65, :], in_=O[64:65, :]), reads=[ok], writes=["at_r1"])
            p.mm(ps[5][0:64, :], onesF[64:65, 0:64], r1[64:65, :], True, True, reads=["at_r1", "onesF"], writes=[PK[5]])
            p.copy("act", bcS[0:64, :], ps[5][0:64, :], reads=[PK[5]], writes=["at_bc"])
            p.tt("dve", dst, O[0:64, :], bcS[0:64, :], ALU.mult, reads=[ok, "at_bc"], writes=[dk])

        def phase_B(l, g):
            S = g.S
            lam_init = 0.8 - 0.6 * math.exp(-0.3 * l)
            with ExitStack() as ctx:
                QT = sbt(ctx, "B_QT", [34, 2, S], BF16)
                KT = sbt(ctx, "B_KT", [34, 2, 2, S], BF16)
                Vt = sbt(ctx, "B_V", [128, S // 128, 4, 65], BF16)
                acol = sbt(ctx, "B_acol", [128, 4, 127], F32)
                aband = sbt(ctx, "B_aband", [128, 4, 896], F32)
                qaugF = sbt(ctx, "B_qaugF", [34, 512], F32)
                r1 = sbt(ctx, "B_r1", [65, 512], F32)
                bcS = sbt(ctx, "B_bc", [64, 512], F32)
                on = [sbt(ctx, f"B_on{i}", [64, 512], F32) for i in range(2)]
                od = sbt(ctx, "B_od", [64, 512], F32)
                sq = sbt(ctx, "B_sq", [64, 512], F32)
                rs = sbt(ctx, "B_rs", [64, 512], F32)
                yb = Ring([sbt(ctx, f"B_y{i}", [64, 512], BF16) for i in range(2)], "B_y")
                lamr = sbt(ctx, "B_lamr", [1, 132], F32)
                nlam = sbt(ctx, "B_nlam", [64, 1], F32)
                gcol = sbt(ctx, "B_gcol", [64, 1], F32)
                p.dma("q_sp", acol[:], C["acol"], writes=["acol"])
                p.dma("q_sp", aband[:], C["aband"], writes=["aband"])
                p.dma("q_sp", lamr[0:1, 0:128], W["diff_lambda"][l].rearrange("(o a) b -> o (a b)", o=1), writes=["lamr"], allow_slow_non_contiguous=True)
                p.tt("dve", lamr[0:1, 0:32], lamr[0:1, 0:32], lamr[0:1, 32:64], ALU.mult, reads=["lamr"], writes=["lamr"])
                p.tt("dve", lamr[0:1, 64:96], lamr[0:1, 64:96], lamr[0:1, 96:128], ALU.mult, reads=["lamr"], writes=["lamr"])
                p.op("dve", lambda e: e.tensor_reduce(out=lamr[0:1, 128:129], in_=lamr[0:1, 0:32], axis=AX.X, op=ALU.add), reads=["lamr"], writes=["lamr"])
                p.op("dve", lambda e: e.tensor_reduce(out=lamr[0:1, 129:130], in_=lamr[0:1, 64:96], axis=AX.X, op=ALU.add), reads=["lamr"], writes=["lamr"])
                p.act(lamr[0:1, 128:130], lamr[0:1, 128:130], AF.Exp, reads=["lamr"], writes=["lamr"])
                p.tt("dve", lamr[0:1, 130:131], lamr[0:1, 129:130], lamr[0:1, 128:129], ALU.subtract, reads=["lamr"], writes=["lamr"])
                p.ts("dve", lamr[0:1, 130:131], lamr[0:1, 130:131], -lam_init, None, ALU.add, reads=["lamr"], writes=["lamr"])
                p.mm(ps[5][0:64, 0:1], onesF[0:1, 0:64], lamr[0:1, 130:131], True, True, reads=["lamr", "onesF"], writes=[PK[5]])
                p.copy("dve", nlam[:], ps[5][0:64, 0:1], reads=[PK[5]], writes=["nlam"])
                p.dma("q_sp", gcol[:].rearrange("p (c o) -> p c o", o=1), W["diff_subln_g"][l].rearrange("(c p o) -> p c o", p=64, o=1), writes=["gcol"], allow_slow_non_contiguous=True)
                p.ts("dve", gcol[:], gcol[:], 1.0 - lam_init, None, ALU.mult, reads=["gcol"], writes=["gcol"])
                for s in range(g.nseq):
                    tok0 = s * S
                    for hv in range(4):
                        KS = min(8, S // 128)
                        for k0 in range(0, S // 128, KS):
                            p.dma("q_sp", Vt[:, k0:k0 + KS, hv, 0:64],
                                  g.vA[tok0 + k0 * 128:tok0 + (k0 + KS) * 128, hv * 64:(hv + 1) * 64].rearrange("(kt p) d -> p kt d", p=128),
                                  writes=["B_V"])
                    p.memset("pool", Vt[:, :, :, 64:65], 1.0, writes=["B_V"])
                    for h in range(4):
                        p.dma("q_pool", qaugF[32:34, :], C["qaug"][h], writes=["B_qaugF"])
                        for j in range(2):
                            r0 = h * 64 + j * 32
                            p.dma("q_sp", QT[0:32, j, :], g.uT[O_AQ + r0:O_AQ + r0 + 32, tok0:tok0 + S], writes=["B_QT"])
                            p.op("pool", lambda e: e.tensor_copy(out=QT[32:34, j, :].rearrange("p (a b) -> p a b", b=512),
                                                                 in_=qaugF[32:34, :].unsqueeze(1).to_broadcast([2, S // 512, 512])),
                                 reads=["B_qaugF"], writes=["B_QT"])
                            for v in range(2):
                                p.dma("q_sp", KT[0:32, j, v, :], g.uT[O_AK + r0:O_AK + r0 + 32, tok0:tok0 + S], writes=["B_KT"])
                                p.memset("pool", KT[32:34, j, v, :], 1.0 if v == 0 else -1.0, writes=["B_KT"])
                        maps = []
                        for j in range(2):
                            maps.append(dict(
                                head=h, acol=acol, aband=aband,
                                q=lambda qt, j=j: QT[0:34, j, qt * 512:(qt + 1) * 512],
                                qd=lambda qt, j=j: QT[0:32, j, qt * 512:(qt + 1) * 512],
                                k=lambda kt, v, j=j: KT[0:34, j, v, kt * 128:(kt + 1) * 128],
                                kd=lambda kt, j=j: KT[0:32, j, 0, kt * 128:(kt + 1) * 128],
                                v=lambda kt, h=h: Vt[:, kt, h, :],
                                keys=["B_QT", "B_KT"], vkeys=["B_V"]))

                        def post(qt, mi, O, ok, h=h, tok0=tok0):
                            attn_norm((r1, bcS), O, ok, on[mi][:], f"B_on{mi}")
                            if mi == 1:
                                p.stt(od[:], on[1][:], nlam[:, 0:1], on[0][:], ALU.mult, ALU.add,
                                      reads=["B_on0", "B_on1", "nlam"], writes=["B_od"])
                                p.act(sq[:], od[:], AF.Square, reads=["B_od"], writes=["B_sq"])
                                p.mm(ps[6][0:64, :], onesF[0:64, 0:64], sq[:], True, True, reads=["B_sq", "onesF"], writes=[PK[6]])
                                rstd_from(rs[:], ps[6][0:64, :], 1.0 / 64, epsc[0:64, 0:1], [PK[6]], ["B_rs"])
                                yt, yk = yb.next()
                                p.stt(yt[:], od[:], gcol[:, 0:1], rs[:], ALU.mult, ALU.mult, reads=["B_od", "gcol", "B_rs"], writes=[yk])
                                p.dma("q_sp", g.yT[h * 64:(h + 1) * 64, tok0 + qt * 512:tok0 + (qt + 1) * 512], yt[:], reads=[yk], writes=())

                        attn_core(ctx, S, maps, post, "B") if False else _attn(ctx, S, maps, post, "B")
            p.barrier()

        _attn_cache = {}

        def _attn(ctx, S, maps, post, pref):
            key = id(ctx)
            if key not in _attn_cache:
                _attn_cache.clear()
                _attn_cache[key] = dict(
                    pT=Ring([sbt(ctx, f"{pref}_pT{i}", [128, 512], BF16) for i in range(4)], pref + "_pT"),
                    tmpF=Ring([sbt(ctx, f"{pref}_tf{i}", [128, 512], F32) for i in range(2)], pref + "_tf"),
                    sring=Ring([ps[0], ps[1], ps[2]], "psS"), oring=Ring([ps[3], ps[4]], "psO"))
            R = _attn_cache[key]
            pT, tmpF, sring, oring = R["pT"], R["tmpF"], R["sring"], R["oring"]
            skeys = {"psS0": PK[0], "psS1": PK[1], "psS2": PK[2], "psO0": PK[3], "psO1": PK[4]}
            nq = S // 512
            nk = S // 128
            for qt in range(nq):
                for mi, m in enumerate(maps):
                    O, okr = oring.next()
                    ok = skeys[okr]
                    pend = None
                    for kt in range(nk + 1):
                        cur = None
                        if kt < nk:
                            Sb, skr = sring.next()
                            sk = skeys[skr]
                            Dd = qt * 512 - kt * 128
                            h = m["head"]
                            if h is None:
                                cls = "plain"
                            elif Dd >= 128:
                                cls = "below"
                            elif Dd <= -512:
                                cls = "above"
                            else:
                                cls = "diag"
                            if cls == "plain":
                                p.mm(Sb[:, :], m["k"](kt, 0), m["q"](qt), True, True, reads=m["keys"], writes=[sk])
                            elif cls == "diag":
                                p.mm(Sb[:, :], m["kd"](kt), m["qd"](qt), True, True, reads=m["keys"], writes=[sk])
                            else:
                                p.mm(Sb[:, :], m["k"](kt, 0 if cls == "below" else 1), m["q"](qt), True, True, reads=m["keys"], writes=[sk])
                            cur = (Sb, sk, cls, Dd, kt)
                        if pend is not None:
                            Sb2, sk2, cls2, Dd2, kt2 = pend
                            pt, pk = pT.next()
                            if cls2 == "plain":
                                p.act(pt[:], Sb2[:, :], AF.Exp, reads=[sk2], writes=[pk])
                            elif cls2 == "diag":
                                tf, tk = tmpF.next()
                                off = Dd2 + 384
                                p.tt("dve", tf[:], Sb2[:, :], m["aband"][:, m["head"], off:off + 512], ALU.add,
                                     reads=[sk2, "aband"], writes=[tk])
                                p.act(pt[:], tf[:], AF.Exp, reads=[tk], writes=[pk])
                            else:
                                jd = Dd2 // 128
                                p.act(pt[:], Sb2[:, :], AF.Exp, reads=[sk2, "acol"], writes=[pk],
                                      bias=m["acol"][:, m["head"], jd + 63:jd + 64])
                            p.mm(O[0:65, :], m["v"](kt2), pt[:], kt2 == 0, kt2 == nk - 1, reads=[pk] + m["vkeys"], writes=[ok])
                        pend = cur
                    post(qt, mi, O, ok)

        def phase_C(l, g):
            S = g.S
            with ExitStack() as ctx:
                wuq = sbt(ctx, "C_wuq", [128, 2, 192], BF16)
                wsw = sbt(ctx, "C_wsw", [128, 2, 4, 48], BF16)
                wukv = sbt(ctx, "C_wukv", [128, 384], BF16)
                gq = sbt(ctx, "C_gq", [128, 2], F32)
                gkv = sbt(ctx, "C_gkv", [128, 1], F32)
                cq = sbt(ctx, "C_cq", [128, 2, 512], BF16)
                ckv = sbt(ctx, "C_ckv", [128, 512], BF16)
                sqb = sbt(ctx, "C_sq", [128, 2, 512], BF16)
                rsd = sbt(ctx, "C_rs", [128, 512], F32)
                cqn = sbt(ctx, "C_cqn", [128, 2, 512], BF16)
                ckvn = sbt(ctx, "C_ckvn", [128, S], BF16)
                QT = sbt(ctx, "C_QT", [48, 2, S], BF16)
                KT = sbt(ctx, "C_KT", [48, 2, S], BF16)
                Vt = sbt(ctx, "C_V", [128, S // 128, 4, 65], BF16)
                rope = sbt(ctx, "C_rope", [48, 4, 512], F32)
                t1 = sbt(ctx, "C_t1", [48, 512], F32)
                t2 = sbt(ctx, "C_t2", [48, 512], F32)
                kr = sbt(ctx, "C_kr", [48, 2, 512], BF16)
                r1 = sbt(ctx, "C_r1", [65, 512], F32)
                bcS = sbt(ctx, "C_bc", [64, 512], F32)
                on = sbt(ctx, "C_on", [64, 512], F32)
                yb = Ring([sbt(ctx, f"C_y{i}", [64, 512], BF16) for i in range(2)], "C_y")
                p.memset("dve", wuq[:], 0.0, writes=["C_wuq"])
                p.memset("dve", wsw[:], 0.0, writes=["C_wsw"])
                p.dma("q_pool", wuq[:, 0, :], W["mla_w_uq"][l, 0:128, :], writes=["C_wuq"])
                p.dma("q_pool", wuq[0:64, 1, :], W["mla_w_uq"][l, 128:192, :], writes=["C_wuq"])
                for hh in range(4):
                    for (dst0, src0) in ((32, 40), (40, 32)):
                        p.dma("q_pool", wsw[:, 0, hh, dst0:dst0 + 8], W["mla_w_uq"][l, 0:128, hh * 48 + src0:hh * 48 + src0 + 8], writes=["C_wsw"])
                        p.dma("q_pool", wsw[0:64, 1, hh, dst0:dst0 + 8], W["mla_w_uq"][l, 128:192, hh * 48 + src0:hh * 48 + src0 + 8], writes=["C_wsw"])
                p.dma("q_pool", wukv[:], W["mla_w_ukv"][l], writes=["C_wukv"])
                p.memset("dve", gq[:], 0.0, writes=["C_gq"])
                p.dma("q_sp", gq[:, 0:1], W["mla_q_norm_g"][l, 0:128].rearrange("(p o) -> p o", o=1), writes=["C_gq"], allow_slow_non_contiguous=True)
                p.dma("q_sp", gq[0:64, 1:2], W["mla_q_norm_g"][l, 128:192].rearrange("(p o) -> p o", o=1), writes=["C_gq"], allow_slow_non_contiguous=True)
                p.dma("q_sp", gkv[:, 0:1], W["mla_kv_norm_g"][l].rearrange("(p o) -> p o", o=1), writes=["C_gkv"], allow_slow_non_contiguous=True)
                p.memset("dve", cq[:], 0.0, writes=["C_cq"])
                p.memset("pool", Vt[:, :, :, 64:65], 1.0, writes=["C_V"])
                for s in range(g.nseq):
                  tok0 = s * S
                  for hp in range(2):
                        for qt in range(S // 512):
                            c0 = tok0 + qt * 512
                            sl = slice(qt * 512, (qt + 1) * 512)
                            p.dma("q_sp", rope[:], C["rope"][:, :, qt * 512:(qt + 1) * 512].rearrange("a p t -> p a t"), writes=["C_rope"])
                            p.dma("q_sp", cq[:, 0, :], g.uT[O_CQ:O_CQ + 128, c0:c0 + 512], writes=["C_cq"])
                            p.dma("q_sp", cq[0:64, 1, :], g.uT[O_CQ + 128:O_CQ + 192, c0:c0 + 512], writes=["C_cq"])
                            p.dma("q_sp", ckv[:], g.uT[O_CKV:O_CKV + 128, c0:c0 + 512], writes=["C_ckv"])
                            p.tt("dve", sqb[:], cq[:], cq[:], ALU.mult, reads=["C_cq"], writes=["C_sq"])
                            p.mm(ps[5][:, :], onesB[:, :], sqb[:, 0, :], True, False, reads=["C_sq", "onesB"], writes=[PK[5]])
                            p.mm(ps[5][:, :], onesB[0:64, :], sqb[0:64, 1, :], False, True, reads=["C_sq", "onesB"], writes=[PK[5]])
                            rstd_from(rsd[:], ps[5][:, :], 1.0 / 192, epsc[:, 0:1], [PK[5]], ["C_rs"])
                            for c in range(2):
                                p.stt(cqn[:, c, :], cq[:, c, :], gq[:, c:c + 1], rsd[:], ALU.mult, ALU.mult, reads=["C_cq", "C_gq", "C_rs"], writes=["C_cqn"])
                            p.tt("dve", sqb[:, 0, :], ckv[:], ckv[:], ALU.mult, reads=["C_ckv"], writes=["C_sq"])
                            p.mm(ps[5][:, :], onesB[:, :], sqb[:, 0, :], True, True, reads=["C_sq", "onesB"], writes=[PK[5]])
                            rstd_from(rsd[:], ps[5][:, :], 1.0 / 128, epsc[:, 0:1], [PK[5]], ["C_rs"])
                            p.stt(ckvn[:, sl], ckv[:], gkv[:, 0:1], rsd[:], ALU.mult, ALU.mult, reads=["C_ckv", "C_gkv", "C_rs"], writes=["C_ckvn"])
                            for hh in (2 * hp, 2 * hp + 1):
                                p.mm(ps[6][0:48, :], wuq[:, 0, hh * 48:(hh + 1) * 48], cqn[:, 0, :], True, False, reads=["C_wuq", "C_cqn"], writes=[PK[6]])
                                p.mm(ps[6][0:48, :], wuq[0:64, 1, hh * 48:(hh + 1) * 48], cqn[0:64, 1, :], False, True, reads=["C_wuq", "C_cqn"], writes=[PK[6]])
                                p.mm(ps[7][0:48, :], wsw[:, 0, hh, :], cqn[:, 0, :], True, False, reads=["C_wsw", "C_cqn"], writes=[PK[7]])
                                p.mm(ps[7][0:48, :], wsw[0:64, 1, hh, :], cqn[0:64, 1, :], False, True, reads=["C_wsw", "C_cqn"], writes=[PK[7]])
                                p.tt("dve", t1[:], ps[6][0:48, :], rope[:, 0, :], ALU.mult, reads=[PK[6], "C_rope"], writes=["C_t1"])
                                p.tt("dve", t2[:], ps[7][0:48, :], rope[:, 1, :], ALU.mult, reads=[PK[7], "C_rope"], writes=["C_t2"])
                                p.tt("pool", QT[:, hh % 2, sl], t1[:], t2[:], ALU.add, reads=["C_t1", "C_t2"], writes=["C_QT"])
                            for hh in (2 * hp, 2 * hp + 1):
                                p.mm(ps[6][0:32, :], wukv[:, hh * 96:hh * 96 + 32], ckvn[:, sl], True, True, reads=["C_wukv", "C_ckvn"], writes=[PK[6]])
                                p.copy("act", KT[0:32, hh % 2, sl], ps[6][0:32, :], reads=[PK[6]], writes=["C_KT"])
                            p.dma("q_sp", kr[32:48, 0, :], g.uT[O_CKR:O_CKR + 16, c0:c0 + 512], writes=["C_kr"])
                            p.dma("q_sp", kr[32:40, 1, :], g.uT[O_CKR + 8:O_CKR + 16, c0:c0 + 512], writes=["C_kr"])
                            p.dma("q_sp", kr[40:48, 1, :], g.uT[O_CKR:O_CKR + 8, c0:c0 + 512], writes=["C_kr"])
                            p.tt("dve", t1[32:48, :], kr[32:48, 0, :], rope[32:48, 2, :], ALU.mult, reads=["C_kr", "C_rope"], writes=["C_t1"])
                            p.tt("dve", t2[32:48, :], kr[32:48, 1, :], rope[32:48, 3, :], ALU.mult, reads=["C_kr", "C_rope"], writes=["C_t2"])
                            for hh in range(2):
                                p.tt("pool", KT[32:48, hh, sl], t1[32:48, :], t2[32:48, :], ALU.add, reads=["C_t1", "C_t2"], writes=["C_KT"])
                            for sub in (range(4) if hp == 0 else ()):
                                kt = qt * 4 + sub
                                for hh in range(4):
                                    p.mm(ps[5][:, hh * 64:(hh + 1) * 64], ckvn[:, qt * 512 + sub * 128:qt * 512 + (sub + 1) * 128],
                                         wukv[:, hh * 96 + 32:hh * 96 + 96], True, True, reads=["C_wukv", "C_ckvn"], writes=[PK[5]])
                                p.copy("act", Vt[:, kt, :, 0:64], ps[5][:, 0:256].rearrange("p (h d) -> p h d", h=4), reads=[PK[5]], writes=["C_V"])
                        for hh in (2 * hp, 2 * hp + 1):
                            maps = [dict(head=None,
                                         q=lambda qt, hh=hh: QT[:, hh % 2, qt * 512:(qt + 1) * 512],
                                         k=lambda kt, v, hh=hh: KT[:, hh % 2, kt * 128:(kt + 1) * 128],
                                         v=lambda kt, hh=hh: Vt[:, kt, hh, :],
                                         keys=["C_QT", "C_KT"], vkeys=["C_V"])]

                            def post(qt, mi, O, ok, hh=hh, tok0=tok0):
                                attn_norm((r1, bcS), O, ok, on[:], "C_on")
                                yt, yk = yb.next()
                                p.copy("pool", yt[:], on[:], reads=["C_on"], writes=[yk])
                                p.dma("q_sp", g.yT[512 + hh * 64:512 + (hh + 1) * 64, tok0 + qt * 512:tok0 + (qt + 1) * 512], yt[:], reads=[yk], writes=())

                            _attn(ctx, S, maps, post, "C")
            p.barrier()

        def phase_D(l, g):
            S = g.S
            NB = min(S, 1024)
            nchb = NB // HC
            with ExitStack() as ctx:
                lb = sbt(ctx, "D_lb", [128, 2, 2, 2], F32)
                lbl = sbt(ctx, "D_lbl", [128, 2, 2, 2], F32)
                hm = sbt(ctx, "D_hm", [64, 2, 4, 64], F32)
                cm = sbt(ctx, "D_cm", [128, NB], F32)
                gn = sbt(ctx, "D_gn", [64, 4], F32)
                zb = sbt(ctx, "D_z", [128, 2, NB], BF16)
                qb = sbt(ctx, "D_q", [128, 2, NB], BF16)
                qs = sbt(ctx, "D_qs", [128, 2, NB], F32)
                f = sbt(ctx, "D_f", [128, NB], F32)
                kk = sbt(ctx, "D_k", [128, NB], F32)
                lf = sbt(ctx, "D_lf", [128, NB], F32)
                cum = sbt(ctx, "D_cum", [128, NB], F32)
                e1 = sbt(ctx, "D_e1", [128, NB], F32)
                qtl = sbt(ctx, "D_qtl", [128, 2, NB], BF16)
                ktl = sbt(ctx, "D_ktl", [128, 2, NB], BF16)
                qh = sbt(ctx, "D_qh", [128, 2, NB], BF16)
                kh = sbt(ctx, "D_kh", [128, 2, NB], BF16)
                dd = sbt(ctx, "D_dd", [128, 2, nchb], F32)
                vt = sbt(ctx, "D_v", [64, nchb, 256], BF16)
                am = Ring([sbt(ctx, f"D_am{i}", [64, 256], BF16) for i in range(2)], "D_am")
                khT = Ring([sbt(ctx, f"D_khT{i}", [64, 2, 128], BF16) for i in range(2)], "D_khT")
                Sf = [sbt(ctx, f"D_S{i}", [128, 2, 64], F32) for i in range(2)]
                Sb16 = [sbt(ctx, f"D_Sb{i}", [128, 2, 64], BF16) for i in range(2)]
                ostg = sbt(ctx, "D_ostg", [64, 4, NB], F32)
                p.dma("q_sp", hm[:], C["hmask"], writes=["D_hm"])
                p.memset("dve", cm[:], 1.0, writes=["D_cm"])
                p.memset("dve", cm[:].rearrange("p (c t) -> p c t", t=HC)[:, :, 0:1], 0.0, writes=["D_cm"])
                p.dma("q_sp", gn[:].rearrange("p (h o) -> p h o", o=1), W["hgrn_norm_g"][l].rearrange("(h p o) -> p h o", p=64, o=1), writes=["D_gn"], allow_slow_non_contiguous=True)
                if l == 0:
                    p.memset("dve", lb[:, :, :, 0:1], 0.0, writes=["D_lb"])
                    p.memset("dve", lb[:, :, :, 1:2], 1.0, writes=["D_lb"])
                else:
                    for ll in range(2):
                        for d_ in range(2):
                            p.dma("q_sp", lbl[:, ll, d_, :].rearrange("p (c o) -> p c o", o=1),
                                  W["hgrn_lb_logits"][ll, d_].rearrange("(c p o) -> p c o", p=128, o=1), writes=["D_lbl"], allow_slow_non_contiguous=True)
                    p.tt("dve", lb[:, :, :, 0], lbl[:, 1, :, :], lbl[:, 0, :, :], ALU.subtract, reads=["D_lbl"], writes=["D_lb"])
                    p.act(lb[:, :, :, 0], lb[:, :, :, 0], AF.Sigmoid, reads=["D_lb"], writes=["D_lb"])
                    p.ts("dve", lb[:, :, :, 1], lb[:, :, :, 0], -1.0, 1.0, ALU.mult, ALU.add, reads=["D_lb"], writes=["D_lb"])
                for s in range(g.nseq):
                    tok0 = s * S
                    for d_ in range(2):
                        for pr in range(2):
                            p.memset("dve", Sf[pr][:], 0.0, writes=[f"D_S{pr}"])
                            p.memset("pool", Sb16[pr][:], 0.0, writes=[f"D_Sb{pr}"])
                        blocks = list(range(0, S, NB))
                        if d_ == 1:
                            blocks = blocks[::-1]
                        zoff = O_BFF if d_ == 0 else O_BFB
                        for b0 in blocks:
                            c0 = tok0 + b0
                            p.dma("q_sp", zb[:], g.uT[zoff:zoff + 256, c0:c0 + NB].rearrange("(c p) t -> p c t", p=128), writes=["D_z"])
                            p.dma("q_sp", qb[:], g.uT[O_BQ:O_BQ + 256, c0:c0 + NB].rearrange("(c p) t -> p c t", p=128), writes=["D_q"])
                            p.dma("q_sp", vt[:], g.vB[c0:c0 + NB, :].rearrange("(c s) f -> s c f", s=HC), writes=["D_v"])
                            p.act(qs[:], qb[:], AF.Silu, reads=["D_q"], writes=["D_qs"])
                            for pr in range(2):
                                p.act(f[:], zb[:, pr, :], AF.Sigmoid, reads=["D_z"], writes=["D_f"])
                                p.ts("dve", f[:], f[:], lb[:, d_, pr, 1:2], lb[:, d_, pr, 0:1], ALU.mult, ALU.add, reads=["D_f", "D_lb"], writes=["D_f"])
                                p.ts("pool", kk[:], f[:], -1.0, 1.0, ALU.mult, ALU.add, reads=["D_f"], writes=["D_k"])
                                p.act(lf[:], f[:], AF.Ln, reads=["D_f"], writes=["D_lf"])
                                p.op("dve", lambda e: e.tensor_tensor_scan(out=cum[:], data0=cm[:], data1=lf[:], initial=0.0, op0=ALU.mult, op1=ALU.add),
                                     reads=["D_cm", "D_lf"], writes=["D_cum"])
                                c3 = cum[:].rearrange("p (c t) -> p c t", t=HC)
                                l3 = lf[:].rearrange("p (c t) -> p c t", t=HC)
                                e3 = e1[:].rearrange("p (c t) -> p c t", t=HC)
                                if d_ == 1:
                                    p.tt("dve", e3, c3[:, :, HC - 1:HC].to_broadcast([128, nchb, HC]), c3, ALU.subtract, reads=["D_cum"], writes=["D_e1"])
                                    p.tt("dve", cum[:], e1[:], lf[:], ALU.add, reads=["D_e1", "D_lf"], writes=["D_cum"])
                                far = HC - 1 if d_ == 0 else 0
                                mid = HC // 2 - 1 if d_ == 0 else HC // 2
                                p.act(dd[:, pr, :].unsqueeze(2), c3[:, :, far:far + 1], AF.Exp, reads=["D_cum"], writes=["D_dd"])
                                p.act(e1[:], cum[:], AF.Exp, reads=["D_cum"], writes=["D_e1"])
                                p.tt("dve", qh[:, pr, :], qs[:, pr, :], e1[:], ALU.mult, reads=["D_qs", "D_e1"], writes=["D_qh"])
                                p.tt("dve", e3, c3[:, :, far:far + 1].to_broadcast([128, nchb, HC]), c3, ALU.subtract, reads=["D_cum"], writes=["D_e1"])
                                p.act(e1[:], e1[:], AF.Exp, reads=["D_e1"], writes=["D_e1"])
                                p.tt("dve", kh[:, pr, :], kk[:], e1[:], ALU.mult, reads=["D_k", "D_e1"], writes=["D_kh"])
                                p.tt("dve", e3, c3, c3[:, :, mid:mid + 1].to_broadcast([128, nchb, HC]), ALU.subtract, reads=["D_cum"], writes=["D_e1"])
                                p.act(lf[:], e1[:], AF.Exp, reads=["D_e1"], writes=["D_lf"])
                                p.tt("dve", qtl[:, pr, :], qs[:, pr, :], lf[:], ALU.mult, reads=["D_qs", "D_lf"], writes=["D_qtl"])
                                p.act(lf[:], e1[:], AF.Exp, reads=["D_e1"], writes=["D_lf"], scale=-1.0)
                                p.tt("dve", ktl[:, pr, :], kk[:], lf[:], ALU.mult, reads=["D_k", "D_lf"], writes=["D_ktl"])
                            chs = list(range(nchb))
                            if d_ == 1:
                                chs = chs[::-1]
                            for ci in chs:
                                cs = slice(ci * HC, (ci + 1) * HC)
                                for hh in range(4):
                                    pr, pb = hh // 2, (hh % 2) * 64
                                    p.mm(ps[0][0:64, hh * 64:(hh + 1) * 64], ktl[pb:pb + 64, pr, cs], qtl[pb:pb + 64, pr, cs], True, True,
                                         reads=["D_ktl", "D_qtl"], writes=[PK[0]])
                                a_t, a_k = am.next()
                                p.tt("dve", a_t[:], ps[0][0:64, 0:256], hm[:, d_, :, :].rearrange("p h t -> p (h t)"), ALU.mult,
                                     reads=[PK[0], "D_hm"], writes=[a_k])
                                for hh in range(4):
                                    pr, pb = hh // 2, (hh % 2) * 64
                                    p.mm(ps[1][0:64, hh * 64:(hh + 1) * 64], vt[:, ci, hh * 64:(hh + 1) * 64], a_t[:, hh * 64:(hh + 1) * 64], True, False,
                                         reads=["D_v", a_k], writes=[PK[1]])
                                    p.mm(ps[1][0:64, hh * 64:(hh + 1) * 64], Sb16[pr][pb:pb + 64, hh % 2, :], qh[pb:pb + 64, pr, cs], False, True,
                                         reads=[f"D_Sb{pr}", "D_qh"], writes=[PK[1]])
                                p.copy("act", ostg[:, :, cs], ps[1][0:64, 0:256].rearrange("p (h t) -> p h t", h=4), reads=[PK[1]], writes=["D_ostg"])
                                kt_t, kt_k = khT.next()
                                for pr in range(2):
                                    p.tr(ps[2][0:64, pr * 64:pr * 64 + 64].bitcast(BF16) if False else psb2[0:64, pr * 128:(pr + 1) * 128], kh[:, pr, cs], identB[:, :],
                                         reads=["D_kh", "identB"], writes=[PK[2]])
                                p.copy("dve", kt_t[:].rearrange("p a b -> p (a b)"), psb2[0:64, 0:256], reads=[PK[2]], writes=[kt_k])
                                for pr in range(2):
                                    p.mm(ps[3][:, pr * 128:(pr + 1) * 128], kt_t[:, pr, :], vt[:, ci, pr * 128:(pr + 1) * 128], True, True,
                                         reads=[kt_k, "D_v"], writes=[PK[3]])
                                for pr in range(2):
                                    for hb in range(2):
                                        pb = hb * 64
                                        p.stt(Sf[pr][pb:pb + 64, hb, :], Sf[pr][pb:pb + 64, hb, :], dd[pb:pb + 64, pr, ci:ci + 1],
                                              ps[3][pb:pb + 64, pr * 128 + hb * 64:pr * 128 + hb * 64 + 64], ALU.mult, ALU.add,
                                              reads=[f"D_S{pr}", "D_dd", PK[3]], writes=[f"D_S{pr}"])
                                    p.copy("act", Sb16[pr][:], Sf[pr][:], reads=[f"D_S{pr}"], writes=[f"D_Sb{pr}"])
                            p.dma("q_sp", g.oFB[d_, :, c0:c0 + NB].rearrange("(h v) t -> v h t", h=4), ostg[:], reads=["D_ostg"], writes=())
                p.barrier()
            with ExitStack() as ctx:
                gn = sbt(ctx, "D2_gn", [64, 4], F32)
                oa = sbt(ctx, "D2_oa", [64, 4, 512], F32)
                ob = sbt(ctx, "D2_ob", [64, 4, 512], F32)
                sq = sbt(ctx, "D2_sq", [64, 4, 512], F32)
                rs = sbt(ctx, "D2_rs", [64, 512], F32)
                bg = sbt(ctx, "D2_bg", [64, 4, 512], BF16)
                sg = sbt(ctx, "D2_sg", [64, 4, 512], F32)
                yb = Ring([sbt(ctx, f"D2_y{i}", [64, 4, 512], BF16) for i in range(2)], "D2_y")
                p.dma("q_sp", gn[:].rearrange("p (h o) -> p h o", o=1), W["hgrn_norm_g"][l].rearrange("(h p o) -> p h o", p=64, o=1), writes=["D2_gn"], allow_slow_non_contiguous=True)
                for t0 in range(0, g.T, 512):
                    p.dma("q_sp", oa[:], g.oFB[0, :, t0:t0 + 512].rearrange("(h v) t -> v h t", h=4), writes=["D2_oa"])
                    p.dma("q_sp", ob[:], g.oFB[1, :, t0:t0 + 512].rearrange("(h v) t -> v h t", h=4), writes=["D2_ob"])
                    p.dma("q_sp", bg[:], g.uT[O_BG:O_BG + 256, t0:t0 + 512].rearrange("(h v) t -> v h t", h=4), writes=["D2_bg"])
                    p.tt("pool", oa[:], oa[:], ob[:], ALU.add, reads=["D2_oa", "D2_ob"], writes=["D2_oa"])
                    p.act(sq[:], oa[:], AF.Square, reads=["D2_oa"], writes=["D2_sq"])
                    p.act(sg[:], bg[:], AF.Silu, reads=["D2_bg"], writes=["D2_sg"])
                    yt, yk = yb.next()
                    for hh in range(4):
                        b = 4 + (hh % 2)
                        p.mm(ps[b][0:64, :], onesF[0:64, 0:64], sq[:, hh, :], True, True, reads=["D2_sq", "onesF"], writes=[PK[b]])
                        rstd_from(rs[:], ps[b][0:64, :], 1.0 / 64, epsc[0:64, 0:1], [PK[b]], ["D2_rs"])
                        p.stt(ob[:, hh, :], oa[:, hh, :], gn[:, hh:hh + 1], rs[:], ALU.mult, ALU.mult, reads=["D2_oa", "D2_gn", "D2_rs"], writes=["D2_ob"])
                        p.tt("pool", yt[:, hh, :], ob[:, hh, :], sg[:, hh, :], ALU.mult, reads=["D2_ob", "D2_sg"], writes=[yk])
                    p.dma("q_sp", g.yT[256:512, t0:t0 + 512].rearrange("(h v) t -> v h t", h=4), yt[:], reads=[yk], writes=())
            p.barrier()

        psb2 = ps[2][:, :].bitcast(BF16)

        def phase_E(l, g):
            S = g.S
            NB = min(S, 2048)
            with ExitStack() as ctx:
                cw = sbt(ctx, "E_cw", [128, 2, 4], F32)
                cb = sbt(ctx, "E_cb", [128, 2], F32)
                ba = sbt(ctx, "E_ba", [128, 2, 2], F32)
                bx = sbt(ctx, "E_bx", [128, 2, 2], F32)
                lam = sbt(ctx, "E_lam", [128, 2, 2], F32)
                wa = sbt(ctx, "E_wa", [128, 2, 2, 128], BF16)
                wx = sbt(ctx, "E_wx", [128, 2, 2, 128], BF16)
                xpad = sbt(ctx, "E_xpad", [128, NB + 4], BF16)
                xc = sbt(ctx, "E_xc", [128, NB], F32)
                xcb = sbt(ctx, "E_xcb", [128, NB], BF16)
                r = sbt(ctx, "E_r", [128, NB], F32)
                ii = sbt(ctx, "E_i", [128, NB], F32)
                a = sbt(ctx, "E_a", [128, NB], F32)
                u = sbt(ctx, "E_u", [128, NB], F32)
                hst = sbt(ctx, "E_h", [128, 2, S], F32)
                carry = sbt(ctx, "E_carry", [128, 1], F32)
                dg = sbt(ctx, "E_dg", [128, NB], BF16)
                gl = sbt(ctx, "E_gl", [128, NB], F32)
                yb = Ring([sbt(ctx, f"E_y{i}", [128, NB], BF16) for i in range(2)], "E_y")
                for j in range(4):
                    p.dma("q_sp", cw[:, :, j:j + 1], W["rg_conv_w"][l, j].rearrange("(c p o) -> p c o", p=128, o=1), writes=["E_cw"], allow_slow_non_contiguous=True)
                p.dma("q_sp", cb[:].rearrange("p (c o) -> p c o", o=1), W["rg_conv_b"][l].rearrange("(c p o) -> p c o", p=128, o=1), writes=["E_cb"], allow_slow_non_contiguous=True)
                for d_ in range(2):
                    p.dma("q_sp", ba[:, d_, :].rearrange("p (c o) -> p c o", o=1), W["rg_b_a"][l, d_].rearrange("(c p o) -> p c o", p=128, o=1), writes=["E_ba"], allow_slow_non_contiguous=True)
                    p.dma("q_sp", bx[:, d_, :].rearrange("p (c o) -> p c o", o=1), W["rg_b_x"][l, d_].rearrange("(c p o) -> p c o", p=128, o=1), writes=["E_bx"], allow_slow_non_contiguous=True)
                    p.dma("q_sp", lam[:, d_, :].rearrange("p (c o) -> p c o", o=1), W["rg_lambda"][l, d_].rearrange("(c p o) -> p c o", p=128, o=1), writes=["E_lam"], allow_slow_non_contiguous=True)
                p.act(lam[:], lam[:], AF.Exp, reads=["E_lam"], writes=["E_lam"], scale=-1.0)
                p.ts("dve", lam[:], lam[:], 1.0, None, ALU.add, reads=["E_lam"], writes=["E_lam"])
                p.act(lam[:], lam[:], AF.Ln, reads=["E_lam"], writes=["E_lam"])
                p.ts("dve", lam[:], lam[:], -8.0, None, ALU.mult, reads=["E_lam"], writes=["E_lam"])
                p.memset("dve", wa[:], 0.0, writes=["E_wa"])
                p.memset("dve", wx[:], 0.0, writes=["E_wx"])
                for d_ in range(2):
                    for n in range(4):
                        pr, pb = n // 2, (n % 2) * 64
                        p.dma("q_pool", wa[pb:pb + 64, d_, pr, pb:pb + 64], W["rg_w_a"][l, d_, n], writes=["E_wa"])
                        p.dma("q_pool", wx[pb:pb + 64, d_, pr, pb:pb + 64], W["rg_w_x"][l, d_, n], writes=["E_wx"])
                for s in range(g.nseq):
                    tok0 = s * S
                    for pr in range(2):
                        rows = slice(O_DX + pr * 128, O_DX + (pr + 1) * 128)
                        for d_ in range(2):
                            blocks = list(range(0, S, NB))
                            if d_ == 1:
                                blocks = blocks[::-1]
                            for bi, b0 in enumerate(blocks):
                                p.memset("dve", xpad[:], 0.0, writes=["E_xpad"])
                                lo = max(b0 - 2, 0)
                                hi = min(b0 + NB + 1, S)
                                p.dma("q_sp", xpad[:, lo - (b0 - 2):hi - (b0 - 2)], g.uT[rows, tok0 + lo:tok0 + hi], writes=["E_xpad"])
                                p.ts("dve", xc[:], xpad[:, 0:NB], cw[:, pr, 0:1], cb[:, pr:pr + 1], ALU.mult, ALU.add, reads=["E_xpad", "E_cw", "E_cb"], writes=["E_xc"])
                                for j in range(1, 4):
                                    p.stt(xc[:], xpad[:, j:j + NB], cw[:, pr, j:j + 1], xc[:], ALU.mult, ALU.add, reads=["E_xpad", "E_cw", "E_xc"], writes=["E_xc"])
                                p.copy("pool", xcb[:], xc[:], reads=["E_xc"], writes=["E_xcb"])
                                for t0 in range(0, NB, 512):
                                    ts_ = slice(t0, t0 + 512)
                                    p.mm(ps[0][:, :], wa[:, d_, pr, :], xcb[:, ts_], True, True, reads=["E_wa", "E_xcb"], writes=[PK[0]])
                                    p.mm(ps[1][:, :], wx[:, d_, pr, :], xcb[:, ts_], True, True, reads=["E_wx", "E_xcb"], writes=[PK[1]])
                                    p.act(r[:, ts_], ps[0][:, :], AF.Sigmoid, reads=[PK[0], "E_ba"], writes=["E_r"], bias=ba[:, d_, pr:pr + 1])
                                    p.act(ii[:, ts_], ps[1][:, :], AF.Sigmoid, reads=[PK[1], "E_bx"], writes=["E_i"], bias=bx[:, d_, pr:pr + 1])
                                p.act(a[:], r[:], AF.Exp, reads=["E_r", "E_lam"], writes=["E_a"], scale=lam[:, d_, pr:pr + 1])
                                p.tt("dve", u[:], a[:], a[:], ALU.mult, reads=["E_a"], writes=["E_u"])
                                p.ts("dve", u[:], u[:], -1.0, 1.0, ALU.mult, ALU.add, reads=["E_u"], writes=["E_u"])
                                p.act(u[:], u[:], AF.Sqrt, reads=["E_u"], writes=["E_u"])
                                p.tt("dve", u[:], u[:], ii[:], ALU.mult, reads=["E_u", "E_i"], writes=["E_u"])
                                p.tt("dve", u[:], u[:], xc[:], ALU.mult, reads=["E_u", "E_xc"], writes=["E_u"])
                                init = 0.0 if bi == 0 else carry[:, 0:1]
                                hdst = hst[:, d_, b0:b0 + NB]
                                if d_ == 0:
                                    p.op("dve", lambda e: e.tensor_tensor_scan(out=hdst, data0=a[:], data1=u[:], initial=init, op0=ALU.mult, op1=ALU.add),
                                         reads=["E_a", "E_u", "E_carry"], writes=["E_h"])
                                    p.copy("dve", carry[:], hst[:, d_, b0 + NB - 1:b0 + NB], reads=["E_h"], writes=["E_carry"])
                                else:
                                    p.op("dve", lambda e: e.tensor_tensor_scan(out=hdst[:, ::-1], data0=a[:, ::-1], data1=u[:, ::-1], initial=init, op0=ALU.mult, op1=ALU.add),
                                         reads=["E_a", "E_u", "E_carry"], writes=["E_h"])
                                    p.copy("dve", carry[:], hst[:, d_, b0:b0 + 1], reads=["E_h"], writes=["E_carry"])
                        for b0 in range(0, S, NB):
                            p.dma("q_sp", dg[:], g.uT[O_DG + pr * 128:O_DG + (pr + 1) * 128, tok0 + b0:tok0 + b0 + NB], writes=["E_dg"])
                            p.act(gl[:], dg[:], AF.Gelu_apprx_tanh, reads=["E_dg"], writes=["E_gl"])
                            p.tt("pool", u[:], hst[:, 0, b0:b0 + NB], hst[:, 1, b0:b0 + NB], ALU.add, reads=["E_h"], writes=["E_u"])
                            yt, yk = yb.next()
                            p.tt("dve", yt[:], u[:], gl[:], ALU.mult, reads=["E_u", "E_gl"], writes=[yk])
                            p.dma("q_sp", g.yT[768 + pr * 128:768 + (pr + 1) * 128, tok0 + b0:tok0 + b0 + NB], yt[:], reads=[yk], writes=())
            p.barrier()

        def layer_norm_tile(tin, tk, tout, ok, gB, bB, st, mv, tmp):
            for hh in range(2):
                p.op("dve", lambda e: e.bn_stats(out=st[:, hh, :], in_=tin[:, hh * 512:(hh + 1) * 512]), reads=[tk], writes=["ln_st"])
            p.op("dve", lambda e: e.bn_aggr(out=mv[:, 0:2], in_=st[:].rearrange("p a b -> p (a b)")), reads=["ln_st"], writes=["ln_mv"])
            rstd_from(mv[:, 2:3], mv[:, 1:2], 1.0, epsc[:, 1:2], ["ln_mv"], ["ln_mv"])
            p.ts("dve", tmp[:], tin[:], mv[:, 0:1], mv[:, 2:3], ALU.subtract, ALU.mult, reads=[tk, "ln_mv"], writes=["ln_tmp"])
            p.tt("pool", tmp[:], tmp[:], gB[:], ALU.mult, reads=["ln_tmp", "lnG"], writes=["ln_tmp"])
            p.tt("dve", tout, tmp[:], bB[:], ALU.add, reads=["ln_tmp", "lnB"], writes=[ok])

        def phase_F(l, g, src):
            T = g.T
            alpha = 4.0 ** 0.25
            with ExitStack() as ctx:
                wb = sbt(ctx, "F_wb", [128, 8, 1024], BF16)
                wo = sbt(ctx, "F_wo", [128, 8, 1024], BF16)
                wrt = sbt(ctx, "F_wr", [128, 8, NE], F32)
                gB = sbt(ctx, "F_gB", [128, 1024], F32)
                bB = sbt(ctx, "F_bB", [128, 1024], F32)
                yt_ = sbt(ctx, "F_y", [128, 8, 512], BF16)
                gt = Ring([sbt(ctx, f"F_g{i}", [128, 4, 512], BF16) for i in range(2)], "F_g")
                tm = Ring([sbt(ctx, f"F_t{i}", [128, 512], F32) for i in range(2)], "F_t")
                acc = sbt(ctx, "F_acc", [128, 512], F32)
                mixT = sbt(ctx, "F_mix", [128, 8, 512], BF16)
                xin = Ring([sbt(ctx, f"F_x{i}", [128, 1024], F32) for i in range(2)], "F_x")
                tsum = sbt(ctx, "F_ts", [128, 1024], F32)
                tmp = sbt(ctx, "F_tmp", [128, 1024], F32)
                xo = Ring([sbt(ctx, f"F_xo{i}", [128, 1024], F32) for i in range(2)], "F_xo")
                xob = Ring([sbt(ctx, f"F_xob{i}", [128, 1024], BF16) for i in range(2)], "F_xob")
                xoT = sbt(ctx, "F_xoT", [128, 8, 128], F32)
                st = sbt(ctx, "F_st", [128, 2, 6], F32)
                mv = sbt(ctx, "F_mv", [128, 4], F32)
                lg = sbt(ctx, "F_lg", [128, NE], F32)
                sm = sbt(ctx, "F_sm", [128, 4], F32)
                af = Ring([sbt(ctx, f"F_af{i}", [128, NE], F32) for i in range(2)], "F_af")
                p.dma("q_pool", wb[:], W["w_branch"][l].rearrange("i (c p) n -> p (i c) n", p=128), writes=["F_wb"])
                p.dma("q_pool", wo[:], W["w_out"][l].rearrange("(c p) n -> p c n", p=128), writes=["F_wo"])
                p.dma("q_sp", wrt[:], W["w_router"][l].rearrange("(c p) n -> p c n", p=128), writes=["F_wr"])
                p.dma("q_sp", gB[:], W["ln_g"][l, 0].partition_broadcast(128), writes=["lnG"])
                p.dma("q_sp", bB[:], W["ln_b"][l, 0].partition_broadcast(128), writes=["lnB"])
                for t0 in range(0, T, 512):
                    p.dma("q_sp", yt_[:], g.yT[:, t0:t0 + 512].rearrange("(c p) t -> p c t", p=128), writes=["F_y"])
                    for oc in range(8):
                        g_t, g_k = gt.next()
                        p.dma("q_sp", g_t[:], g.uT[O_GATE:O_GATE + 4096, t0:t0 + 512].rearrange("(i c p) t -> c p i t", i=4, p=128)[oc], writes=[g_k])
                        for i in range(4):
                            b = i % 4
                            for kc in range(2):
                                p.mm(ps[b][:, :], wb[:, i * 2 + kc, oc * 128:(oc + 1) * 128], yt_[:, i * 2 + kc, :], kc == 0, kc == 1,
                                     reads=["F_wb", "F_y"], writes=[PK[b]])
                            if i == 0:
                                p.tt("dve", acc[:], ps[b][:, :], g_t[:, i, :], ALU.mult, reads=[PK[b], g_k], writes=["F_acc"])
                            else:
                                t_t, t_k = tm.next()
                                p.tt("dve", t_t[:], ps[b][:, :], g_t[:, i, :], ALU.mult, reads=[PK[b], g_k], writes=[t_k])
                                if i < 3:
                                    p.tt("pool", acc[:], acc[:], t_t[:], ALU.add, reads=["F_acc", t_k], writes=["F_acc"])
                                else:
                                    p.tt("pool", mixT[:, oc, :], acc[:], t_t[:], ALU.add, reads=["F_acc", t_k], writes=["F_mix"])
                    for sub in range(4):
                        r0 = t0 + sub * 128
                        x_t, x_k = xin.next()
                        p.dma("q_sp", x_t[:], src[r0:r0 + 128, :], writes=[x_k])
                        for hf in range(2):
                            b = 4 + hf
                            for kc in range(8):
                                p.mm(ps[b][:, :], mixT[:, kc, sub * 128:(sub + 1) * 128], wo[:, kc, hf * 512:(hf + 1) * 512], kc == 0, kc == 7,
                                     reads=["F_mix", "F_wo"], writes=[PK[b]])
                            p.stt(tsum[:, hf * 512:(hf + 1) * 512], x_t[:, hf * 512:(hf + 1) * 512], alpha, ps[b][:, :], ALU.mult, ALU.add,
                                  reads=[x_k, PK[b]], writes=["F_ts"])
                        xo_t, xo_k = xo.next()
                        layer_norm_tile(tsum, "F_ts", xo_t[:], xo_k, gB, bB, st, mv, tmp)
                        p.dma("q_sp", g.xs[r0:r0 + 128, :], xo_t[:], reads=[xo_k], writes=())
                        xb_t, xb_k = xob.next()
                        p.copy("act", xb_t[:], xo_t[:], reads=[xo_k], writes=[xb_k])
                        p.dma("q_sp", g.x1b[r0:r0 + 128, :], xb_t[:], reads=[xb_k], writes=())
                        for b in range(2):
                            for j in range(4):
                                kc = b * 4 + j
                                p.tr(ps[6 + b][:, j * 128:(j + 1) * 128], xo_t[:, kc * 128:(kc + 1) * 128], identF[:], reads=[xo_k, "identF"], writes=[PK[6 + b]])
                            p.copy("act", xoT[:, b * 4:(b + 1) * 4, :], ps[6 + b][:].rearrange("p (j t) -> p j t", j=4), reads=[PK[6 + b]], writes=["F_xoT"])
                        for kc in range(8):
                            p.mm(ps[6][:, 0:NE], xoT[:, kc, :], wrt[:, kc, :], kc == 0, kc == 7, reads=["F_xoT", "F_wr"], writes=[PK[6]])
                        p.copy("dve", lg[:], ps[6][:, 0:NE], reads=[PK[6]], writes=["F_lg"])
                        p.op("dve", lambda e: e.tensor_reduce(out=sm[:, 0:1], in_=lg[:], axis=AX.X, op=ALU.max), reads=["F_lg"], writes=["F_sm"])
                        p.ts("dve", sm[:, 1:2], sm[:, 0:1], -1.0, None, ALU.mult, reads=["F_sm"], writes=["F_sm"])
                        p.act(lg[:], lg[:], AF.Exp, reads=["F_lg", "F_sm"], writes=["F_lg"], bias=sm[:, 1:2])
                        p.op("dve", lambda e: e.tensor_reduce(out=sm[:, 2:3], in_=lg[:], axis=AX.X, op=ALU.add), reads=["F_lg"], writes=["F_sm"])
                        p.op("dve", lambda e: e.reciprocal(out=sm[:, 3:4], in_=sm[:, 2:3]), reads=["F_sm"], writes=["F_sm"])
                        a_t, a_k = af.next()
                        p.ts("dve", a_t[:], lg[:], sm[:, 3:4], None, ALU.mult, reads=["F_lg", "F_sm"], writes=[a_k])
                        p.dma("q_sp", g.affL[r0:r0 + 128, :], a_t[:], reads=[a_k], writes=())
            p.barrier()

        def phase_G(l, g, last):
            T, J, CM = g.T, g.J, g.cmax
            JG = NCORES * T // 128
            alpha = 4.0 ** 0.25
            p.op("pool", lambda e: e.collective_compute("AllGather", ALU.bypass, replica_groups=[list(range(NCORES))], ins=[g.affL], outs=[g.affG]),
                 reads=[], writes=["affG"])
            p.barrier()
            outer = ExitStack()
            slotAll = sbt(outer, "G_slotAll", [128, NE, CM // 128, 2], I32)
            with ExitStack() as ctx:
                ag = sbt(ctx, "G_ag", [128, JG, NE], F32)
                cmpt = sbt(ctx, "G_cmp", [128, JG, NE], F32)
                lohi = sbt(ctx, "G_lohi", [128, 4, NE], F32)
                cnt = sbt(ctx, "G_cnt", [128, NE], F32)
                ge = sbt(ctx, "G_ge", [128, NE], F32)
                al = sbt(ctx, "G_al", [128, J, NE], F32)
                msk = sbt(ctx, "G_msk", [128, NE, J], F32)
                pos = sbt(ctx, "G_pos", [128, NE, J], F32)
                tok = sbt(ctx, "G_tok", [128, J], I32)
                tot = sbt(ctx, "G_tot", [128, NE], F32)
                offs = sbt(ctx, "G_offs", [128, NE], F32)
                ltri = sbt(ctx, "G_ltri", [128, 128], F32)
                onesJ = sbt(ctx, "G_onesJ", [128, J], F32)
                p.dma("q_sp", ag[:], g.affG.rearrange("(p j) e -> p j e", p=128), writes=["G_ag"])
                p.dma("q_sp", al[:], g.affL.rearrange("(p j) e -> p j e", p=128), writes=["G_al"])
                p.dma("q_sp", ltri[:], C["ltri"], writes=["G_ltri"])
                p.dma("q_sp", tok[:], g.tokid, writes=["G_tok"])
                p.memset("dve", lohi[:, 0, :], 0.0, writes=["G_lohi"])
                p.memset("dve", lohi[:, 1, :], 1.0, writes=["G_lohi"])
                p.memset("dve", onesJ[:], 1.0, writes=["G_onesJ"])
                for it in range(30):
                    p.tt("dve", lohi[:, 2, :], lohi[:, 0, :], lohi[:, 1, :], ALU.add, reads=["G_lohi"], writes=["G_lohi"])
                    p.ts("dve", lohi[:, 2, :], lohi[:, 2, :], 0.5, None, ALU.mult, reads=["G_lohi"], writes=["G_lohi"])
                    p.tt("dve", cmpt[:], ag[:], lohi[:, 2, :].unsqueeze(1).to_broadcast([128, JG, NE]), ALU.is_ge, reads=["G_ag", "G_lohi"], writes=["G_cmp"])
                    p.op("dve", lambda e: e.tensor_reduce(out=cnt[:], in_=cmpt[:].rearrange("p j e -> p e j"), axis=AX.X, op=ALU.add), reads=["G_cmp"], writes=["G_cnt"])
                    p.mm(ps[0][:, 0:NE], onesF[:, :], cnt[:], True, True, reads=["G_cnt", "onesF"], writes=[PK[0]])
                    p.ts("dve", ge[:], ps[0][:, 0:NE], float(g.cap), None, ALU.is_ge, reads=[PK[0]], writes=["G_ge"])
                    p.tt("dve", lohi[:, 3, :], lohi[:, 2, :], lohi[:, 0, :], ALU.subtract, reads=["G_lohi"], writes=["G_lohi"])
                    p.tt("dve", lohi[:, 3, :], lohi[:, 3, :], ge[:], ALU.mult, reads=["G_lohi", "G_ge"], writes=["G_lohi"])
                    p.tt("dve", lohi[:, 0, :], lohi[:, 0, :], lohi[:, 3, :], ALU.add, reads=["G_lohi"], writes=["G_lohi"])
                    p.ts("dve", ge[:], ge[:], -1.0, 1.0, ALU.mult, ALU.add, reads=["G_ge"], writes=["G_ge"])
                    p.tt("dve", lohi[:, 3, :], lohi[:, 2, :], lohi[:, 1, :], ALU.subtract, reads=["G_lohi"], writes=["G_lohi"])
                    p.tt("dve", lohi[:, 3, :], lohi[:, 3, :], ge[:], ALU.mult, reads=["G_lohi", "G_ge"], writes=["G_lohi"])
                    p.tt("dve", lohi[:, 1, :], lohi[:, 1, :], lohi[:, 3, :], ALU.add, reads=["G_lohi"], writes=["G_lohi"])
                p.tt("dve", msk[:], al[:].rearrange("p j e -> p e j"), lohi[:, 0, :].unsqueeze(2).to_broadcast([128, NE, J]), ALU.is_ge,
                     reads=["G_al", "G_lohi"], writes=["G_msk"])
                for e_ in range(NE):
                    p.op("dve", lambda e, e_=e_: e.tensor_tensor_scan(out=pos[:, e_, :], data0=onesJ[:], data1=msk[:, e_, :], initial=0.0, op0=ALU.mult, op1=ALU.add),
                         reads=["G_msk", "G_onesJ"], writes=["G_pos"])
                p.copy("dve", tot[:], pos[:, :, J - 1], reads=["G_pos"], writes=["G_tot"])
                p.mm(ps[1][:, 0:NE], ltri[:, :], tot[:], True, True, reads=["G_ltri", "G_tot"], writes=[PK[1]])
                p.ts("dve", offs[:], ps[1][:, 0:NE], -1.0, None, ALU.add, reads=[PK[1]], writes=["G_offs"])
                p.tt("dve", pos[:], pos[:], offs[:].unsqueeze(2).to_broadcast([128, NE, J]), ALU.add, reads=["G_pos", "G_offs"], writes=["G_pos"])
                p.ts("dve", pos[:], pos[:], -1.0e6, None, ALU.add, reads=["G_pos"], writes=["G_pos"])
                p.tt("dve", pos[:], pos[:], msk[:], ALU.mult, reads=["G_pos", "G_msk"], writes=["G_pos"])
                p.ts("dve", pos[:], pos[:], 1.0e6, None, ALU.add, reads=["G_pos"], writes=["G_pos"])
                NTs = CM // 128
                iotaS = sbt(ctx, "G_iotaS", [128, CM], F32)
                trash = sbt(ctx, "G_trash", [128, NTs], F32)
                val = sbt(ctx, "G_val", [128, J, NE, 3], F32)
                oh = Ring([sbt(ctx, f"G_oh{i}", [128, J, 128], F32) for i in range(2)], "G_oh")
                res = sbt(ctx, "G_res", [128, NTs, 4], F32)
                idf = sbt(ctx, "G_idf", [128, NTs], F32)
                p.dma("q_sp", iotaS[:], g.iotaS, writes=["G_iotaS"])
                p.dma("q_sp", trash[:], g.trash, writes=["G_trash"])
                p.copy("dve", val[:, :, :, 0], tok[:].unsqueeze(2).to_broadcast([128, J, NE]), reads=["G_tok"], writes=["G_val"])
                p.copy("dve", val[:, :, :, 1], al[:], reads=["G_al"], writes=["G_val"])
                p.memset("dve", val[:, :, :, 2], 1.0, writes=["G_val"])
                for e_ in range(NE):
                    b = 2 + (e_ % 2)
                    for sc in range(NTs):
                        o_t, o_k = oh.next()
                        p.tt("dve", o_t[:], iotaS[:, sc * 128:(sc + 1) * 128].unsqueeze(1).to_broadcast([128, J, 128]),
                             pos[:, e_, :].unsqueeze(2).to_broadcast([128, J, 128]), ALU.is_equal, reads=["G_iotaS", "G_pos"], writes=[o_k])
                        for j in range(J):
                            p.mm(ps[b][:, sc * 4:sc * 4 + 3], o_t[:, j, :], val[:, j, e_, :], j == 0, j == J - 1,
                                 reads=[o_k, "G_val"], writes=[PK[b]])
                    p.copy("dve", res[:], ps[b][:, 0:NTs * 4].rearrange("p (s k) -> p s k", k=4), reads=[PK[b]], writes=["G_res"])
                    p.ts("dve", idf[:], res[:, :, 2], -1.0, 1.0, ALU.mult, ALU.add, reads=["G_res"], writes=["G_idf"])
                    p.tt("dve", idf[:], idf[:], trash[:], ALU.mult, reads=["G_idf", "G_trash"], writes=["G_idf"])
                    p.tt("dve", idf[:], idf[:], res[:, :, 0], ALU.add, reads=["G_idf", "G_res"], writes=["G_idf"])
                    p.copy("dve", slotAll[:, e_, :, 0], idf[:], reads=["G_idf"], writes=["G_slotAll"])
                    p.copy("dve", slotAll[:, e_, :, 1].bitcast(F32), res[:, :, 1], reads=["G_res"], writes=["G_slotAll"])
            if isinstance(debug, str) and "Z" in debug:
                    p.barrier()
                    NTs_ = CM // 128
                    p.dma("q_sp", g.y_out[0:128, 0:NE * NTs_ * 2], slotAll[:].rearrange("p e t k -> p (e t k)").bitcast(F32), writes=())
                    p.dma("q_sp", g.y_out[128:256, 0:NE], lohi[:, 0, :], writes=())
                    p.dma("q_sp", g.y_out[256:384, 0:J * NE], al[:].rearrange("p j e -> p (j e)"), writes=())
                    p.dma("q_sp", g.y_out[384:512, 0:J * NE], pos[:].rearrange("p e j -> p (e j)"), writes=())
                    p.dma("q_sp", g.y_out[512:640, 0:NE], offs[:], writes=())
                    p.dma("q_sp", g.y_out[640:768, 0:NE], tot[:], writes=())
                    p.dma("q_sp", g.y_out[768:896, 0:J * NE], msk[:].rearrange("p e j -> p (e j)"), writes=())
                    p.barrier()
            if isinstance(debug, str) and "Z" in debug:
                outer.close()
                return
            p.barrier()
            ztx = ExitStack()
            zt = sbt(ztx, "G_zt", [128, D], F32)
            p.memset("dve", zt[:], 0.0, writes=["G_zt"])
            for r0 in range(0, T + CM, 128):
                p.dma("q_sp", g.yacc[r0:r0 + 128, :], zt[:], reads=["G_zt"], writes=["yacc"])
            zb16 = sbt(ztx, "G_zb16", [128, D], BF16)
            p.memset("dve", zb16[:], 0.0, writes=["G_zb16"])
            for r0 in range(T, T + CM, 128):
                p.dma("q_sp", g.x1b[r0:r0 + 128, :], zb16[:], reads=["G_zb16"], writes=())
            p.barrier()
            ztx.close()
            with ExitStack() as ctx:
                NT = CM // 128
                wg = sbt(ctx, "G_wg", [128, 8, DE], BF16)
                wu = sbt(ctx, "G_wu", [128, 8, DE], BF16)
                wd = sbt(ctx, "G_wd", [128, 16, D], BF16)
                xe = Ring([sbt(ctx, f"G_xe{i}", [128, D], BF16) for i in range(2)], "G_xe")
                xeT = sbt(ctx, "G_xeT", [128, 8, CM], BF16)
                hT = sbt(ctx, "G_hT", [128, 16, CM], BF16)
                sg = Ring([sbt(ctx, f"G_sg{i}", [128, 512], F32) for i in range(2)], "G_sg")
                ye = Ring([sbt(ctx, f"G_ye{i}", [128, D], F32) for i in range(2)], "G_ye")
                ttiles = [(t0, min(512, CM - t0)) for t0 in range(0, CM, 512)]
                for e_ in range(NE):
                    p.dma("q_pool", wg[:], W["w_e_gate"][l, e_].rearrange("(c p) n -> p c n", p=128), writes=["G_wg"])
                    p.dma("q_pool", wu[:], W["w_e_up"][l, e_].rearrange("(c p) n -> p c n", p=128), writes=["G_wu"])
                    p.dma("q_pool", wd[:], W["w_e_down"][l, e_].rearrange("(c p) n -> p c n", p=128), writes=["G_wd"])
                    slot = slotAll[:, e_, :, :]
                    for ti in range(NT):
                        x_t, x_k = xe.next()
                        p.op("q_gather", lambda e, ti=ti, x_t=x_t: e.indirect_dma_start(
                            out=x_t[:], out_offset=None, in_=g.x1b, in_offset=bass.IndirectOffsetOnAxis(ap=slot[:, ti, 0:1], axis=0)),
                            reads=["G_slotAll"], writes=[x_k])
                        pb = psb2 if (ti % 2 == 0) else psb3
                        pk = PK[2] if (ti % 2 == 0) else PK[3]
                        for kc in range(8):
                            p.tr(pb[:, kc * 128:(kc + 1) * 128], x_t[:, kc * 128:(kc + 1) * 128], identB[:, :], reads=[x_k, "identB"], writes=[pk])
                        p.copy("dve" if ti % 2 == 0 else "act", xeT[:, :, ti * 128:(ti + 1) * 128], pb[:, :].rearrange("p (c t) -> p c t", c=8), reads=[pk], writes=["G_xeT"])
                    fi = 0
                    for fc in range(16):
                        for (t0, tw) in ttiles:
                            ba, bb = ((0, 1), (6, 7))[fi % 2]
                            fi += 1
                            for kc in range(8):
                                p.mm(ps[ba][:, 0:tw], wg[:, kc, fc * 128:(fc + 1) * 128], xeT[:, kc, t0:t0 + tw], kc == 0, kc == 7, reads=["G_wg", "G_xeT"], writes=[PK[ba]])
                            for kc in range(8):
                                p.mm(ps[bb][:, 0:tw], wu[:, kc, fc * 128:(fc + 1) * 128], xeT[:, kc, t0:t0 + tw], kc == 0, kc == 7, reads=["G_wu", "G_xeT"], writes=[PK[bb]])
                            s_t, s_k = sg.next()
                            p.act(s_t[:, 0:tw], ps[ba][:, 0:tw], AF.Silu, reads=[PK[ba]], writes=[s_k])
                            p.tt("dve", hT[:, fc, t0:t0 + tw], s_t[:, 0:tw], ps[bb][:, 0:tw], ALU.mult, reads=[s_k, PK[bb]], writes=["G_hT"])
                    for ti in range(NT):
                        y_t, y_k = ye.next()
                        for hf in range(2):
                            b = 4 + hf
                            for fc in range(16):
                                p.mm(ps[b][:, :], hT[:, fc, ti * 128:(ti + 1) * 128], wd[:, fc, hf * 512:(hf + 1) * 512], fc == 0, fc == 15, reads=["G_hT", "G_wd"], writes=[PK[b]])
                            p.ts("dve" if hf == 0 else "dve", y_t[:, hf * 512:(hf + 1) * 512], ps[b][:, :], slot[:, ti, 1:2].bitcast(F32), None, ALU.mult,
                                 reads=[PK[b], "G_slotAll"], writes=[y_k])
                        p.op("q_scat", lambda e, ti=ti, y_t=y_t: e.indirect_dma_start(
                            out=g.yacc, out_offset=bass.IndirectOffsetOnAxis(ap=slot[:, ti, 0:1], axis=0),
                            in_=y_t[:], in_offset=None, compute_op=ALU.add), reads=["G_slotAll", y_k], writes=["yacc"])
            p.barrier()
            outer.close()
            with ExitStack() as ctx:
                gB = sbt(ctx, "H_gB", [128, 1024], F32)
                bB = sbt(ctx, "H_bB", [128, 1024], F32)
                xin = Ring([sbt(ctx, f"H_x{i}", [128, 1024], F32) for i in range(2)], "H_x")
                yin = Ring([sbt(ctx, f"H_yy{i}", [128, 1024], F32) for i in range(2)], "H_yy")
                tsum = sbt(ctx, "H_ts", [128, 1024], F32)
                tmp = sbt(ctx, "H_tmp", [128, 1024], F32)
                xo = Ring([sbt(ctx, f"H_xo{i}", [128, 1024], F32) for i in range(2)], "H_xo")
                st = sbt(ctx, "H_st", [128, 2, 6], F32)
                mv = sbt(ctx, "H_mv", [128, 4], F32)
                p.dma("q_sp", gB[:], W["ln_g"][l, 1].partition_broadcast(128), writes=["lnG"])
                p.dma("q_sp", bB[:], W["ln_b"][l, 1].partition_broadcast(128), writes=["lnB"])
                dst = g.y_out if last else g.xs
                dumpy = isinstance(debug, str) and "Y" in debug
                for r0 in range(0, T, 128):
                    x_t, x_k = xin.next()
                    y_t, y_k = yin.next()
                    p.dma("q_sp", x_t[:], g.xs[r0:r0 + 128, :], writes=[x_k])
                    p.dma("q_sp", y_t[:], g.yacc[r0:r0 + 128, :], writes=[y_k])
                    if dumpy:
                        p.dma("q_sp", dst[r0:r0 + 128, :], y_t[:], reads=[y_k], writes=())
                        continue
                    p.stt(tsum[:], x_t[:], alpha, y_t[:], ALU.mult, ALU.add, reads=[x_k, y_k], writes=["H_ts"])
                    xo_t, xo_k = xo.next()
                    layer_norm_tile(tsum, "H_ts", xo_t[:], xo_k, gB, bB, st, mv, tmp)
                    p.dma("q_sp", dst[r0:r0 + 128, :], xo_t[:], reads=[xo_k, x_k], writes=())
            p.barrier()

        psb3 = ps[3][:, :].bitcast(BF16)

        stop_after = debug if isinstance(debug, str) else None
        for l in range(depth):
            for g in groups:
                src = g.x_in if l == 0 else g.xs
                PH = debug if isinstance(debug, str) else "ABCDEFG"
                if "A" in PH: phase_A(l, g, src)
                if "B" in PH: phase_B(l, g)
                if "C" in PH: phase_C(l, g)
                if "D" in PH: phase_D(l, g)
                if "E" in PH: phase_E(l, g)
                if "F" in PH: phase_F(l, g, src)
            for g in groups:
                if "G" in PH: phase_G(l, g, l == depth - 1)
        if isinstance(debug, str) and "G" not in debug:
            for g in groups:
                for r0 in range(0, g.T, 1024):
                    p.dma("q_sp", g.y_out[r0:r0 + 1024, :], g.xs[r0:r0 + 1024, :])
        p.barrier()
        print("n_inst", p.n_inst)
    return nc


def make_consts(groups):
    c = {}
    c["c_ident"] = np.eye(128, dtype=np.float32)
    qaug = np.zeros((4, 2, 512), np.float32)
    ql = np.arange(512)
    for h in range(4):
        qaug[h, 0] = -SLOPES[h] * 16.0 * (ql // 16)
        qaug[h, 1] = -SLOPES[h] * (ql % 16)
    c["c_qaug"] = qaug
    kl = np.arange(128)[:, None, None]
    jd = (np.arange(127) - 63)[None, None, :]
    sl = np.array(SLOPES)[None, :, None]
    c["c_acol"] = (np.sign(jd) * sl * (kl - 128.0 * jd)).astype(np.float32)
    x = np.arange(896)[None, None, :]
    c["c_aband"] = (-sl * np.abs(x - 384 - kl)).astype(np.float32)
    s_ = np.arange(64)[:, None]
    t_ = np.arange(64)[None, :]
    hm = np.zeros((64, 2, 4, 64), np.float32)
    hm[:, 0] = (s_ <= t_).astype(np.float32)[:, None, :]
    hm[:, 1] = (s_ >= t_).astype(np.float32)[:, None, :]
    c["c_hmask"] = hm
    c["c_ltri"] = (np.arange(128)[:, None] < np.arange(128)[None, :]).astype(np.float32)
    Smax = max(g.S for g in groups)
    inv = (10000.0 ** (-np.arange(0, 16, 2, dtype=np.float32) / 16)).astype(np.float32)
    ang = (np.arange(Smax, dtype=np.float32)[:, None] * inv[None, :]).astype(np.float32)
    cos, sin = np.cos(ang).T.astype(np.float32), np.sin(ang).T.astype(np.float32)
    rope = np.zeros((4, 48, Smax), np.float32)
    sc = 48.0 ** -0.5
    rope[0, 0:32] = sc
    rope[0, 32:40] = cos * sc
    rope[0, 40:48] = cos * sc
    rope[1, 32:40] = -sin * sc
    rope[1, 40:48] = sin * sc
    rope[2, 0:32] = 1.0
    rope[2, 32:40] = cos
    rope[2, 40:48] = cos
    rope[3, 32:40] = -sin
    rope[3, 40:48] = sin
    c["c_rope"] = rope
    for g in groups:
        c["c_tokid_" + g.name] = (np.arange(128)[:, None] * g.J + np.arange(g.J)[None, :]).astype(np.int32)
        c["c_iotaS_" + g.name] = np.tile(np.arange(g.cmax, dtype=np.float32)[None, :], (128, 1))
        nt = g.cmax // 128
        c["c_trash_" + g.name] = (g.T + np.arange(nt)[None, :] * 128 + np.arange(128)[:, None]).astype(np.float32)
    return c


WEIGHT_NAMES = ["w_in", "diff_lambda", "diff_subln_g", "hgrn_lb_logits", "hgrn_norm_g", "mla_q_norm_g", "mla_w_uq",
                "mla_kv_norm_g", "mla_w_ukv", "rg_conv_w", "rg_conv_b", "rg_w_a", "rg_b_a", "rg_w_x", "rg_b_x",
                "rg_lambda", "w_branch", "w_out", "ln_g", "ln_b", "w_router", "w_e_gate", "w_e_up", "w_e_down"]


def run(inputs, depth=2, debug=False):
    xp = np.ascontiguousarray(inputs["x_prompt"], dtype=np.float32)
    xs = np.ascontiguousarray(inputs["x_sample"], dtype=np.float32)
    Bp, Sp, _ = xp.shape
    Bs, Ss, _ = xs.shape
    groups = [Grp("p", Bp // NCORES, Sp), Grp("s", Bs // NCORES, Ss)]
    nc = build_program(groups, depth, debug)
    consts = make_consts(groups)
    base = {k: np.ascontiguousarray(inputs[k][:depth] if k != "hgrn_lb_logits" else inputs[k][:max(depth, 1)], dtype=np.float32) for k in WEIGHT_NAMES}
    base.update(consts)
    in_maps = []
    for c in range(NCORES):
        m = dict(base)
        m["x_p"] = xp[c * groups[0].nseq:(c + 1) * groups[0].nseq].reshape(-1, D)
        m["x_s"] = xs[c * groups[1].nseq:(c + 1) * groups[1].nseq].reshape(-1, D)
        in_maps.append(m)
    import os
    if os.environ.get("K_TRACE"):
        res = run_bass_kernel_spmd(nc, in_maps, core_ids=list(range(NCORES)), trace=True)
        print("EXEC_NS", res.exec_time_ns)
    else:
        res = run_bass_kernel_spmd(nc, in_maps, core_ids=list(range(NCORES)))
    yp = np.concatenate([res.results[c]["y_p"].reshape(groups[0].nseq, Sp, D) for c in range(NCORES)], 0)
    ys = np.concatenate([res.results[c]["y_s"].reshape(groups[1].nseq, Ss, D) for c in range(NCORES)], 0)
    return yp.astype(np.float32), ys.astype(np.float32)


def kernel(**inputs):
    return run(inputs, depth=2)
```
